# Optimizing a Trainium2 kernel written in Bass

```python
import jax, jax.numpy as jnp
from jax import lax
import numpy as np

D_MODEL = 1024
BATCH = 4
SEQ = 8192
DEPTH = 1
DEC_BATCH = 4
DEC_SEQ = 4096
PAST_LEN = 128

N_META = 16
GRID_W = 64
NA_HEADS = 8
NA_HEAD_DIM = 64
NA_WIDTH = NA_HEADS * NA_HEAD_DIM
NA_WIN_ROWS = 8
NA_WIN_COLS = 16
FN_GROUPS = 4
FN_GROUP_DIM = 128
FN_WIDTH = FN_GROUPS * FN_GROUP_DIM
D_FF = 4 * D_MODEL
RMS_EPS = 1e-6
IN_WIDTH = 3 * NA_WIDTH + FN_WIDTH + 2 * D_MODEL

kernel_name = 'hybrid_natten_fnet_encoder'


def rmsnorm(x, g):
    xf = x.astype(jnp.float32)
    y = xf * lax.rsqrt(jnp.mean(xf * xf, axis=-1, keepdims=True) + RMS_EPS)
    return (y * g.astype(jnp.float32)).astype(x.dtype)


def neighbourhood_attention(q, k, v, rel_bias, meta_bias):
    B, N, H, Dh = q.shape
    T = N - N_META
    rows = T // GRID_W
    wr = min(NA_WIN_ROWS, rows)
    scale = Dh ** -0.5
    qm, km, vm = q[:, :N_META], k[:, :N_META], v[:, :N_META]
    qg = q[:, N_META:].reshape(B, rows, GRID_W, H, Dh)
    kg = k[:, N_META:].reshape(B, rows, GRID_W, H, Dh)
    vg = v[:, N_META:].reshape(B, rows, GRID_W, H, Dh)

    s_mm = jnp.einsum('bqhd,bkhd->bhqk', qm, km) * scale + meta_bias[None, :, None, :]
    p_mm = jax.nn.softmax(s_mm.astype(jnp.float32), axis=-1).astype(v.dtype)
    o_meta = jnp.einsum('bhqk,bkhd->bqhd', p_mm, vm)

    cols = jnp.arange(GRID_W)
    col_start = jnp.clip(cols - NA_WIN_COLS // 2, 0, GRID_W - NA_WIN_COLS)
    col_idx = col_start[:, None] + jnp.arange(NA_WIN_COLS)[None, :]
    col_off = col_idx - cols[:, None]
    col_bias = rel_bias[:, :, col_off + NA_WIN_COLS - 1]

    def row_block(r):
        rs = jnp.clip(r - wr // 2, 0, rows - wr)
        q_r = lax.dynamic_index_in_dim(qg, r, axis=1, keepdims=False)
        k_rows = lax.dynamic_slice_in_dim(kg, rs, wr, axis=1)
        v_rows = lax.dynamic_slice_in_dim(vg, rs, wr, axis=1)
        k_win = k_rows[:, :, col_idx]
        v_win = v_rows[:, :, col_idx]
        row_off = rs + jnp.arange(wr) - r
        bias = col_bias[:, row_off + NA_WIN_ROWS - 1]
        s_loc = jnp.einsum('bchd,bwcjhd->bhcwj', q_r, k_win) * scale + jnp.transpose(bias, (0, 2, 1, 3))[None]
        s_loc = s_loc.reshape(B, H, GRID_W, wr * NA_WIN_COLS)
        s_met = jnp.einsum('bchd,bmhd->bhcm', q_r, km) * scale + meta_bias[None, :, None, :]
        s = jnp.concatenate([s_loc, s_met], axis=-1).astype(jnp.float32)
        p = jax.nn.softmax(s, axis=-1).astype(v.dtype)
        p_loc = p[..., :wr * NA_WIN_COLS].reshape(B, H, GRID_W, wr, NA_WIN_COLS)
        p_met = p[..., wr * NA_WIN_COLS:]
        return (jnp.einsum('bhcwj,bwcjhd->bchd', p_loc, v_win)
                + jnp.einsum('bhcm,bmhd->bchd', p_met, vm))

    o_grid = lax.map(row_block, jnp.arange(rows))
    o_grid = jnp.transpose(o_grid, (1, 0, 2, 3, 4)).reshape(B, T, H, Dh)
    return jnp.concatenate([o_meta, o_grid], axis=1)


def fourier_mix(u):
    B, N, _ = u.shape
    ug = u.reshape(B, N, FN_GROUPS, FN_GROUP_DIM).astype(jnp.float32)
    f = jnp.fft.fft2(ug, axes=(1, 3), norm='ortho').real
    return f.reshape(B, N, FN_WIDTH).astype(u.dtype)


def encoder_layer(x, w_in, rel_bias, meta_bias, w_branch_na, w_branch_fn, w_out, g_mix, g_mlp, w_up, w_down):
    B, N, _ = x.shape
    h = rmsnorm(x, g_mix)
    z = h @ w_in
    q, k, v, u, gate_na, gate_fn = jnp.split(
        z, [NA_WIDTH, 2 * NA_WIDTH, 3 * NA_WIDTH, 3 * NA_WIDTH + FN_WIDTH,
            3 * NA_WIDTH + FN_WIDTH + D_MODEL], axis=-1)
    q = q.reshape(B, N, NA_HEADS, NA_HEAD_DIM)
    k = k.reshape(B, N, NA_HEADS, NA_HEAD_DIM)
    v = v.reshape(B, N, NA_HEADS, NA_HEAD_DIM)
    y_na = neighbourhood_attention(q, k, v, rel_bias, meta_bias).reshape(B, N, NA_WIDTH) @ w_branch_na
    y_fn = fourier_mix(u) @ w_branch_fn
    mixed = jax.nn.sigmoid(gate_na) * y_na + jax.nn.sigmoid(gate_fn) * y_fn
    x = x + mixed @ w_out
    a = jax.nn.relu(rmsnorm(x, g_mlp) @ w_up)
    return x + (a * a) @ w_down


def run_trunk(x, meta_tokens, w_in, rel_bias, meta_bias, w_branch_na, w_branch_fn, w_out,
              g_mix, g_mlp, w_up, w_down, g_final):
    B = x.shape[0]
    meta = jnp.broadcast_to(meta_tokens.astype(x.dtype)[None], (B, N_META, D_MODEL))
    h = jnp.concatenate([meta, x], axis=1)
    for l in range(DEPTH):
        h = encoder_layer(h, w_in[l], rel_bias[l], meta_bias[l], w_branch_na[l], w_branch_fn[l],
                          w_out[l], g_mix[l], g_mlp[l], w_up[l], w_down[l])
    return rmsnorm(h, g_final)[:, N_META:]


def setup_inputs(seed: int = 0) -> dict:
    key = jax.random.key(seed)
    ks = jax.random.split(key, 16)
    L, D = DEPTH, D_MODEL

    def nrm(k, shape, scale):
        return jax.random.normal(k, shape, jnp.float32) * scale

    return {
        'x_prompt': nrm(ks[0], (BATCH, SEQ, D), 1.0),
        'x_sample': nrm(ks[1], (DEC_BATCH, DEC_SEQ, D), 1.0),
        'meta_tokens': nrm(ks[2], (N_META, D), 1.0),
        'w_in': nrm(ks[3], (L, D, IN_WIDTH), D ** -0.5),
        'rel_bias': nrm(ks[4], (L, NA_HEADS, 2 * NA_WIN_ROWS - 1, 2 * NA_WIN_COLS - 1), 0.1),
        'meta_bias': nrm(ks[5], (L, NA_HEADS, N_META), 0.1),
        'w_branch_na': nrm(ks[6], (L, NA_WIDTH, D), NA_WIDTH ** -0.5),
        'w_branch_fn': nrm(ks[7], (L, FN_WIDTH, D), FN_WIDTH ** -0.5),
        'w_out': nrm(ks[8], (L, D, D), D ** -0.5),
        'g_mix': 1.0 + nrm(ks[9], (L, D), 0.02),
        'g_mlp': 1.0 + nrm(ks[10], (L, D), 0.02),
        'w_up': nrm(ks[11], (L, D, D_FF), D ** -0.5),
        'w_down': nrm(ks[12], (L, D_FF, D), D_FF ** -0.5),
        'g_final': 1.0 + nrm(ks[13], (D,), 0.02),
    }


def reference(x_prompt, x_sample, meta_tokens, w_in, rel_bias, meta_bias, w_branch_na, w_branch_fn,
              w_out, g_mix, g_mlp, w_up, w_down, g_final):
    y_prompt = run_trunk(x_prompt, meta_tokens, w_in, rel_bias, meta_bias, w_branch_na, w_branch_fn,
                         w_out, g_mix, g_mlp, w_up, w_down, g_final)
    y_sample = run_trunk(x_sample, meta_tokens, w_in, rel_bias, meta_bias, w_branch_na, w_branch_fn,
                         w_out, g_mix, g_mlp, w_up, w_down, g_final)
    return (y_prompt, y_sample)
```

```python
import numpy as np
import ml_dtypes
from contextlib import ExitStack
import concourse.bass as bass
import concourse.mybir as mybir
from concourse.bass_utils import run_bass_kernel_spmd

F32 = mybir.dt.float32
BF16 = mybir.dt.bfloat16
AF = mybir.ActivationFunctionType
ALU = mybir.AluOpType
NPBF = ml_dtypes.bfloat16

D = 1024
NMETA = 16
GW = 64
NH = 8
DH = 64
DFF = 4096
EPS = 1e-6
NEG = -30000.0
S1 = 0.25


class Grp:
    def __init__(self, name, T):
        self.name = name
        self.T = T
        self.N = T + NMETA
        self.M = self.N // 16
        self.rows_total = T // GW
        self.R = self.rows_total // 2
        self.Th = self.R * GW
        self.NP = self.R // 2
        self.NB = self.Th // 512
        self.blk = (self.M - 1) // 8
        self.nch = (self.M - 1) // 128
        self.nj = self.Th // 16
        self.ncol = 2 * self.nj
        self.next = (self.R + 8) * GW
        self.nkv = self.NP + 4


GP = Grp("p", 8192)
GS = Grp("s", 4096)


def _bias_tiles(rel_bias, meta_bias, g, hf, lr, krow0, nt):
    out = np.full((128, NH, nt + 1, 128), NEG, np.float32)
    q = np.arange(128)
    qr = g.R * hf + lr + q // 64
    qc = q % 64
    rs = np.clip(qr - 4, 0, g.rows_total - 8)
    cs = np.clip(qc - 8, 0, GW - 16)
    k = np.arange(128)
    for i in range(nt):
        kr = g.R * hf + krow0 + 2 * i + k // 64
        kc = k % 64
        valid = ((kr[:, None] >= 0) & (kr[:, None] < g.rows_total)
                 & (kr[:, None] >= rs[None, :]) & (kr[:, None] < rs[None, :] + 8)
                 & (kc[:, None] >= cs[None, :]) & (kc[:, None] < cs[None, :] + 16))
        dr = np.clip(kr[:, None] - qr[None, :] + 7, 0, 14)
        dc = np.clip(kc[:, None] - qc[None, :] + 15, 0, 30)
        vals = rel_bias[:, dr, dc]
        out[:, :, i, :] = np.where(valid[None], vals, np.float32(NEG)).transpose(1, 0, 2)
    out[:NMETA, :, nt, :] = np.broadcast_to(meta_bias.T[:, :, None], (NMETA, NH, 128))
    return out


def _special_pairs(g):
    return [(0, 0, 6), (1, 1, 5), (g.NP - 2, g.NP - 2, 5), (g.NP - 1, g.NP - 2, 6)]


def _host_consts():
    ident = np.eye(128, dtype=np.float32).astype(NPBF)
    n1 = np.arange(16)
    ang = 2 * np.pi * np.outer(n1, n1) / 16.0
    w16 = np.zeros((128, 256), np.float32)
    for j in range(8):
        w16[j * 16:(j + 1) * 16, j * 32:j * 32 + 16] = np.cos(ang) * S1
        w16[j * 16:(j + 1) * 16, j * 32 + 16:j * 32 + 32] = -np.sin(ang) * S1
    c = np.arange(128)
    a2 = 2 * np.pi * np.outer(c, c) / 128.0
    cs = np.stack([np.cos(a2), np.sin(a2)], 1) / np.sqrt(128.0)
    return ident, w16.astype(NPBF), cs.astype(np.float32).astype(NPBF)


def _host_G(g, hf):
    sc = (1.0 / S1) / np.sqrt(float(g.N))
    k1 = np.arange(16)[:, None, None]
    n2 = np.arange(g.M)[None, :, None]
    k2 = (1 + hf * g.nj + np.arange(g.nj))[None, None, :]
    ph = (n2 * (k1 + 16 * k2)) % g.N
    ang = 2 * np.pi * ph.astype(np.float64) / g.N
    Gr = np.cos(ang) * sc
    Gi = -np.sin(ang) * sc
    forAr = np.concatenate([Gr, Gi], -1)
    forAi = np.concatenate([-Gi, Gr], -1)
    both = np.stack([forAr, forAi], 2)
    main = both[:, :g.M - 1].reshape(16, g.nch, 128, 2, g.ncol).transpose(0, 2, 1, 3, 4)
    last = both[:, g.M - 1]
    return (np.ascontiguousarray(main).astype(np.float32).astype(NPBF),
            np.ascontiguousarray(last).astype(np.float32).astype(NPBF))


class Buf:
    __slots__ = ("name", "w", "r", "pw", "pr", "sem", "semv")

    def __init__(self, name):
        self.name = name
        self.w = {}
        self.r = {}
        self.pw = {}
        self.pr = {}
        self.sem = None
        self.semv = 0


class Sy:
    ENG = ("pe", "act", "dve", "pool", "sp")

    def __init__(self, nc, es):
        self.nc = nc
        self.es = es
        self.eng = dict(pe=nc.tensor, act=nc.scalar, dve=nc.vector, pool=nc.gpsimd, sp=nc.sync)
        self.csem = {}
        self.ccnt = {}
        self.waited = {}
        self.nsem = 0
        self.bufs = []
        for e in self.ENG:
            self.csem[e] = self.newsem("c_" + e)
            self.ccnt[e] = 0

    def newsem(self, name):
        self.nsem += 1
        return self.es.enter_context(self.nc.semaphore("%s_%d" % (name, self.nsem)))

    def buf(self, name):
        b = Buf(name)
        self.bufs.append(b)
        return b

    def bufs_n(self, name, n):
        return [self.buf("%s%d" % (name, i)) for i in range(n)]

    def _wait(self, e, evs):
        best = {}
        for ev in evs:
            k = id(ev[0])
            if k not in best or best[k][1] < ev[1]:
                best[k] = ev
        for sem, val, src in best.values():
            if src == e:
                if e == "pe":
                    continue
                if sem is not self.csem[e]:
                    continue
                if self.ccnt[e] - val >= 2:
                    continue
            if self.waited.get((e, id(sem)), 0) >= val:
                continue
            self.eng[e].wait_ge(sem, val)
            self.waited[(e, id(sem))] = val

    def _deps(self, reads, writes, wacc):
        evs = []
        for b in reads:
            evs.extend(b.w.values())
        for b in writes:
            evs.extend(b.w.values())
            evs.extend(b.r.values())
        for b in wacc:
            evs.extend(b.r.values())
            evs.extend(b.pr.values())
            evs.extend(b.pw.values())
        return evs

    @staticmethod
    def _put(d, ev):
        k = id(ev[0])
        if k not in d or d[k][1] < ev[1]:
            d[k] = ev

    def _reg(self, ev, reads, writes, wacc):
        for b in reads:
            self._put(b.r, ev)
        for b in writes:
            b.pw = b.w
            b.pr = b.r
            b.w = {id(ev[0]): ev}
            b.r = {}
        for b in wacc:
            if b.r:
                b.pw = b.w
                b.pr = b.r
                b.w = {}
                b.r = {}
            self._put(b.w, ev)

    def op(self, e, emit, reads=(), writes=(), wacc=()):
        self._wait(e, self._deps(reads, writes, wacc))
        inst = emit()
        if self.ccnt[e] >= 30000:
            self.csem[e] = self.newsem("c_" + e)
            self.ccnt[e] = 0
        self.ccnt[e] += 1
        inst.then_inc(self.csem[e], 1)
        ev = (self.csem[e], self.ccnt[e], e)
        self._reg(ev, reads, writes, wacc)
        return ev

    def dma(self, q, out, in_, reads=(), writes=(), wacc=(), sem_of=None):
        self._wait(q, self._deps(reads, writes, wacc))
        inst = self.eng[q].dma_start(out=out, in_=in_)
        b = sem_of
        if b.sem is None:
            b.sem = self.newsem("d_" + b.name)
        b.semv += 16
        assert b.semv < 60000
        inst.then_inc(b.sem, 16)
        ev = (b.sem, b.semv, None)
        self._reg(ev, reads, writes, wacc)
        return ev

    def barrier(self, engines=None):
        evs = []
        for b in self.bufs:
            evs.extend(b.w.values())
            evs.extend(b.r.values())
            evs.extend(b.pw.values())
            evs.extend(b.pr.values())
        for e in (engines or self.ENG):
            best = {}
            for ev in evs:
                k = id(ev[0])
                if k not in best or best[k][1] < ev[1]:
                    best[k] = ev
            for sem, val, src in best.values():
                if src == e and e == "pe":
                    continue
                if self.waited.get((e, id(sem)), 0) >= val:
                    continue
                self.eng[e].wait_ge(sem, val)
                self.waited[(e, id(sem))] = val


def build_nc(debug=None):
    nc = bass.Bass("TRN2", target_bir_lowering=False)
    top = ExitStack()
    sy = Sy(nc, top)
    T_ = nc.tensor
    A_ = nc.scalar
    V_ = nc.vector
    G_ = nc.gpsimd

    uid = [0]

    def sbt(name, shape, dt):
        uid[0] += 1
        return nc.sbuf_tensor("%s_u%d" % (name, uid[0]), shape, dt)

    def pst(name, shape, dt):
        uid[0] += 1
        return nc.psum_tensor("%s_u%d" % (name, uid[0]), shape, dt)

    def din(name, shape, dt=F32):
        return nc.dram_tensor(name, list(shape), dt, kind="ExternalInput").ap()

    def dscr(name, shape, dt):
        kind = "ExternalOutput" if (debug and name in debug) else "Internal"
        return nc.dram_tensor(name, list(shape), dt, kind=kind).ap()

    w_in = din("w_in", [D, 4096])
    w_na = din("w_na", [512, D])
    w_fn = din("w_fn", [512, D])
    w_out = din("w_out", [D, D])
    w_up = din("w_up", [D, DFF])
    w_dn = din("w_dn", [DFF, D])
    g_mix = din("g_mix", [1, D])
    g_mlp = din("g_mlp", [1, D])
    g_fin = din("g_fin", [1, D])
    meta = din("meta", [NMETA, D])
    identd = din("ident", [128, 128], BF16)
    w16d = din("w16", [128, 256], BF16)
    csd = din("cs", [128, 2, 128], BF16)
    bgen = din("bgen", [128, NH, 6, 128])
    gd = {}
    for g in (GP, GS):
        n = g.name
        gd[n] = dict(
            xs=din("xs_" + n, [g.N, D]),
            xe=din("xe_" + n, [g.next, D]),
            gm=din("gm_" + n, [16, 128, g.nch, 2, g.ncol], BF16),
            gl=din("gl_" + n, [16, 2, g.ncol], BF16),
            bs=din("bs_" + n, [4, 128, NH, 7, 128]),
            y=nc.dram_tensor("y_" + n, [g.Th, D], F32, kind="ExternalOutput").ap(),
            Ud=dscr("Ud_" + n, [g.N, 512], BF16),
            Ad=dscr("Ad_" + n, [2, 16, g.M, 512], BF16),
            Qd=dscr("Qd_" + n, [8, 128, g.Th], BF16),
            hTd=dscr("hTd_" + n, [8, 128, g.Th], BF16),
            oTd=dscr("oTd_" + n, [4, 128, g.Th], BF16),
            x1d=dscr("x1d_" + n, [g.Th, D], F32),
        )
        for k in ("Ud", "Ad", "Qd", "hTd", "oTd", "x1d", "y"):
            gd[n]["B" + k] = sy.buf(k + "_" + n)

    stop_after = (debug or {}).get("stop") if isinstance(debug, dict) else None

    E = top.enter_context
    ident = E(sbt("ident_sb", [128, 128], BF16))
    gbc_mix = E(sbt("gbc_mix", [128, D], F32))
    gbc_mlp = E(sbt("gbc_mlp", [128, D], F32))
    gbc_fin = E(sbt("gbc_fin", [128, D], F32))
    Bconst = sy.buf("const")
    sy.dma("sp", ident[:], identd[:, :], writes=[Bconst], sem_of=Bconst)
    sy.dma("sp", gbc_mix[:], g_mix[0:1, :].partition_broadcast(128), wacc=[Bconst], sem_of=Bconst)
    sy.dma("sp", gbc_mlp[:], g_mlp[0:1, :].partition_broadcast(128), wacc=[Bconst], sem_of=Bconst)
    sy.dma("sp", gbc_fin[:], g_fin[0:1, :].partition_broadcast(128), wacc=[Bconst], sem_of=Bconst)

    def load_w(dst, src_ap, B, nsplit, Bsrc=None):
        kcs = dst.shape[1]
        for kc in range(kcs):
            if Bsrc is None:
                sy.dma("pool", dst[:, kc, :], src_ap[kc * 128:(kc + 1) * 128, :], wacc=[B], sem_of=B)
            else:
                sy.dma("sp", dst[:, kc, :], src_ap[kc * 128:(kc + 1) * 128, :], reads=[Bsrc], wacc=[B], sem_of=B)

    wb = {}

    precast_q = []

    def precast(name, src_ap, rows, cols):
        dst = dscr("wb_" + name, [rows, cols], BF16)
        B = sy.buf("wb_" + name)
        for r in range(0, rows, 128):
            precast_q.append((dst[r:r + 128, :], src_ap[r:r + 128, :], B))
        wb[name] = (dst, B)

    def precast_some(n):
        for _ in range(n):
            if not precast_q:
                return
            o_, i_, B = precast_q.pop(0)
            sy.dma("pool", o_, i_, wacc=[B], sem_of=B)

    ss = [E(sbt("ss%d" % i, [128, 1], F32)) for i in range(2)]
    sd = [E(sbt("sd%d" % i, [128, 1], F32)) for i in range(2)]
    rstd = [E(sbt("rstd%d" % i, [128, 1], F32)) for i in range(2)]
    junk = E(sbt("junk", [128, D], BF16))
    Bss = sy.bufs_n("ss", 2)
    Bsd = sy.bufs_n("sd", 2)
    Brs = sy.bufs_n("rstd", 2)
    Bjunk = sy.buf("junk")
    rms_ctr = [0]
    rms_mode = ["sqrt"]

    def rmsnorm(x_ap, Bx, rows, gbc, out_ap, Bout):
        s = rms_ctr[0] % 2
        rms_ctr[0] += 1
        sy.op("act", lambda: A_.activation(out=junk[0:rows, :], in_=x_ap, func=AF.Square,
                                           accum_out=ss[s][0:rows, :]),
              reads=[Bx], writes=[Bss[s], Bjunk])
        if rms_mode[0] == "lnexp":
            sy.op("act", lambda: A_.activation(out=sd[s][0:rows, :], in_=ss[s][0:rows, :], func=AF.Ln,
                                               scale=1.0 / D, bias=EPS),
                  reads=[Bss[s]], writes=[Bsd[s]])
            sy.op("act", lambda: A_.activation(out=rstd[s][0:rows, :], in_=sd[s][0:rows, :], func=AF.Exp, scale=-0.5),
                  reads=[Bsd[s]], writes=[Brs[s]])
        else:
            sy.op("act", lambda: A_.activation(out=sd[s][0:rows, :], in_=ss[s][0:rows, :], func=AF.Sqrt,
                                               scale=1.0 / D, bias=EPS),
                  reads=[Bss[s]], writes=[Bsd[s]])
            sy.op("dve", lambda: V_.reciprocal(out=rstd[s][0:rows, :], in_=sd[s][0:rows, :]),
                  reads=[Bsd[s]], writes=[Brs[s]])
        sy.op("dve", lambda: V_.scalar_tensor_tensor(out=out_ap, in0=x_ap, scalar=rstd[s][0:rows, 0:1],
                                                     in1=gbc[0:rows, :], op0=ALU.mult, op1=ALU.mult),
              reads=[Bx, Brs[s], Bconst], writes=[Bout])

    cp_ctr = [0]

    def evac(out_ap, in_ap, reads, writes=(), wacc=(), eng=None):
        if eng is None:
            eng = "act" if cp_ctr[0] % 2 == 0 else "dve"
            cp_ctr[0] += 1
        if eng == "act":
            return sy.op("act", lambda: A_.copy(out=out_ap, in_=in_ap), reads=reads, writes=writes, wacc=wacc)
        return sy.op("dve", lambda: V_.tensor_copy(out=out_ap, in_=in_ap), reads=reads, writes=writes, wacc=wacc)

    def transpose_to(pT, BpT, h_sb, Bh, rows, dst_ap_fn, Bdst, nchunk=8, wacc=True, eng=None):
        def em():
            i = None
            for c in range(nchunk):
                i = T_.transpose(out=pT[:, c, 0:rows], in_=h_sb[0:rows, c * 128:(c + 1) * 128],
                                 identity=ident[0:rows, 0:rows])
            return i
        sy.op("pe", em, reads=[Bh, Bconst], writes=[BpT])
        evac(dst_ap_fn(), pT[:, 0:nchunk, 0:rows], reads=[BpT], wacc=[Bdst] if wacc else (), writes=() if wacc else [Bdst],
             eng=eng)

    def phase_F(g):
        d = gd[g.name]
        with ExitStack() as ph:
            P = ph.enter_context
            Wu = P(sbt("Wu", [128, 8, 512], BF16))
            BWu = sy.buf("Wu")
            load_w(Wu, w_in[:, 1536:2048], BWu, 8)
            NXF = 6
            xt = [P(sbt("f_xt%d" % i, [128, D], F32)) for i in range(NXF)]
            hb = [P(sbt("f_hb%d" % i, [128, D], BF16)) for i in range(3)]
            hT = [P(sbt("f_hT%d" % i, [128, 8, 128], BF16)) for i in range(2)]
            Us = [P(sbt("f_Us%d" % i, [128, 512], BF16)) for i in range(3)]
            pT = [P(pst("f_pT%d" % i, [128, 8, 128], BF16)) for i in range(2)]
            pU = [P(pst("f_pU%d" % i, [128, 512], F32)) for i in range(6)]
            Bxt = sy.bufs_n("fxt", NXF)
            Bhb, BhT, BUs, BpT, BpU = (sy.bufs_n(n, k) for n, k in
                                       (("fhb", 3), ("fhT", 2), ("fUs", 3), ("fpT", 2), ("fpU", 6)))
            w16 = P(sbt("w16", [128, 256], BF16))
            Bw16 = sy.buf("w16")
            sy.dma("sp", w16[:], w16d[:, :], writes=[Bw16], sem_of=Bw16)
            SB = 8
            Ast = [P(sbt("f_Ast%d" % i, [128, 2, SB, 512], BF16)) for i in range(2)]
            BAst = sy.bufs_n("fAst", 2)
            xs3 = d["xs"].rearrange("(a m) f -> a m f", a=16)
            Adv = d["Ad"].rearrange("r k m c -> (r k) m c")
            ntile = g.blk + 1
            puc = [0]

            def nextpu():
                k = puc[0] % 6
                puc[0] += 1
                return pU[k], BpU[k]

            def rows_of(t):
                return 128 if t < g.blk else 16

            def f1_L(t):
                if t >= ntile:
                    return
                s = t % NXF
                if t < g.blk:
                    for j in range(8):
                        sy.dma("sp", xt[s][j * 16:(j + 1) * 16, :], xs3[:, j * g.blk + t, :],
                               wacc=[Bxt[s]] if j else (), writes=() if j else [Bxt[s]], sem_of=Bxt[s])
                else:
                    sy.dma("sp", xt[s][0:16, :], xs3[:, g.M - 1, :], writes=[Bxt[s]], sem_of=Bxt[s])

            def f1_N(t):
                if t >= ntile:
                    return
                rows = rows_of(t)
                rmsnorm(xt[t % NXF][0:rows, :], Bxt[t % NXF], rows, gbc_mix, hb[t % 3][0:rows, :], Bhb[t % 3])

            def f1_X(t):
                if t >= ntile:
                    return
                rows = rows_of(t)
                transpose_to(pT[t % 2], BpT[t % 2], hb[t % 3], Bhb[t % 3], rows, lambda: hT[t % 2][:, :, 0:rows], BhT[t % 2],
                             wacc=False, eng="dve")

            def f1_M(t):
                rows = rows_of(t)
                s = t % 2
                pu, Bpu = nextpu()

                def em():
                    i = None
                    for dc in range(8):
                        i = T_.matmul(pu[0:rows, :], lhsT=hT[s][:, dc, 0:rows], rhs=Wu[:, dc, :],
                                      start=(dc == 0), stop=(dc == 7))
                    return i
                sy.op("pe", em, reads=[BhT[s], BWu], writes=[Bpu])
                evac(Us[t % 3][0:rows, :], pu[0:rows, :], reads=[Bpu], writes=[BUs[t % 3]], eng="act")

            def f1_S(t):
                a_s = (t // SB) % 2
                i = t % SB
                if t < g.blk:
                    for half in range(2):
                        pu, Bpu = nextpu()
                        sy.op("pe", lambda: T_.matmul(pu[:, :], lhsT=w16[:, half * 128:(half + 1) * 128],
                                                      rhs=Us[t % 3][:, :], start=True, stop=True),
                              reads=[BUs[t % 3], Bw16], writes=[Bpu])
                        evac(Ast[a_s][:, half, i, :], pu[:, :], reads=[Bpu], wacc=[BAst[a_s]],
                             eng="act" if half == 0 else "dve")
                    if i == SB - 1:
                        sc = t // SB
                        for half in range(2):
                            for jj in range(4):
                                n2a = (half * 4 + jj) * g.blk + sc * SB
                                sy.dma("sp", Adv[:, n2a:n2a + SB, :], Ast[a_s][jj * 32:(jj + 1) * 32, half, :, :],
                                       reads=[BAst[a_s]], wacc=[d["BAd"]], sem_of=BAst[a_s])
                else:
                    pu, Bpu = nextpu()
                    sy.op("pe", lambda: T_.matmul(pu[0:32, :], lhsT=w16[0:16, 0:32], rhs=Us[t % 3][0:16, :],
                                                  start=True, stop=True), reads=[BUs[t % 3], Bw16], writes=[Bpu])
                    evac(Ast[a_s][0:32, 0, 0, :], pu[0:32, :], reads=[Bpu], writes=[BAst[a_s]], eng="act")
                    sy.dma("sp", Adv[:, g.M - 1, :], Ast[a_s][0:32, 0, 0, :], reads=[BAst[a_s]], wacc=[d["BAd"]],
                           sem_of=BAst[a_s])
            PF = NXF - 1
            for t in range(PF):
                f1_L(t)
            f1_N(0)
            f1_N(1)
            f1_X(0)
            for t in range(ntile + 1):
                f1_L(t + PF)
                f1_N(t + 2)
                f1_X(t + 1)
                if t < ntile:
                    f1_M(t)
                if t >= 1:
                    f1_S(t - 1)
            if stop_after in ("F1" + g.name, "F2" + g.name):
                return True
            nch, ncol, nj = g.nch, g.ncol, g.nj
            Ak = [P(sbt("f_Ak%d" % i, [128, nch, 2, 512], BF16)) for i in range(2)]
            Akl = [P(sbt("f_Akl%d" % i, [1, 2, 512], BF16)) for i in range(2)]
            Gk = [P(sbt("f_Gk%d" % i, [128, nch, 2, ncol], BF16)) for i in range(2)]
            Gkl = [P(sbt("f_Gkl%d" % i, [1, 2, ncol], BF16)) for i in range(2)]
            Qst = P(sbt("f_Qst", [128, 8, g.Th], BF16))
            BAk = sy.bufs_n("fAk", 2)
            BGk = sy.bufs_n("fGk", 2)
            BQst = sy.buf("fQst")
            Qv = Qst[:].rearrange("p a (j k) -> p a j k", k=16)
            def f3_load(k1):
                if k1 >= 16:
                    return
                s = k1 % 2
                sy.dma("sp", Gk[s][:], d["gm"][k1], writes=[BGk[s]], sem_of=BGk[s])
                sy.dma("sp", Gkl[s][0:1, :, :], d["gl"][k1:k1 + 1, :, :], wacc=[BGk[s]], sem_of=BGk[s])
                for ri in range(2):
                    sy.dma("sp", Ak[s][:, :, ri, :],
                           d["Ad"][ri, k1, 0:g.M - 1, :].rearrange("(ch p) c -> p ch c", p=128),
                           reads=[d["BAd"]], wacc=[BAk[s]], sem_of=BAk[s])
                sy.dma("sp", Akl[s][0:1, :, :], d["Ad"][:, k1, g.M - 1:g.M, :].rearrange("r o c -> o r c"),
                       reads=[d["BAd"]], wacc=[BAk[s]], sem_of=BAk[s])
            f3_load(0)
            for k1 in range(16):
                s = k1 % 2
                f3_load(k1 + 1)
                for cc in range(4):
                    k = (k1 * 4 + cc) % 4
                    pu = pU[k]

                    def em():
                        first = True
                        for ch in range(nch):
                            for ri in range(2):
                                T_.matmul(pu[:, 0:ncol], lhsT=Ak[s][:, ch, ri, cc * 128:(cc + 1) * 128],
                                          rhs=Gk[s][:, ch, ri, :], start=first, stop=False)
                                first = False
                        T_.matmul(pu[:, 0:ncol], lhsT=Akl[s][0:1, 0, cc * 128:(cc + 1) * 128],
                                  rhs=Gkl[s][0:1, 0, :], start=False, stop=False)
                        return T_.matmul(pu[:, 0:ncol], lhsT=Akl[s][0:1, 1, cc * 128:(cc + 1) * 128],
                                         rhs=Gkl[s][0:1, 1, :], start=False, stop=True)
                    sy.op("pe", em, reads=[BAk[s], BGk[s]], writes=[BpU[k]])
                    evac(Qv[:, cc * 2:cc * 2 + 2, :, k1], pu[:, 0:ncol].rearrange("p (r j) -> p r j", r=2),
                         reads=[BpU[k]], wacc=[BQst])
            sy.dma("sp", d["Qd"].rearrange("a p t -> p a t"), Qst[:], reads=[BQst], writes=[d["BQd"]], sem_of=BQst)
            sy.barrier()
        return False

    def phase_A1():
        rms_mode[0] = "lnexp"
        with ExitStack() as ph:
            P = ph.enter_context
            Wq = P(sbt("Wq", [128, 8, 512], BF16))
            Wk = P(sbt("Wk", [128, 8, 512], BF16))
            Wv = P(sbt("Wv", [128, 8, 512], BF16))
            BWq, BWk, BWv = sy.buf("Wq"), sy.buf("Wk"), sy.buf("Wv")
            load_w(Wk, w_in[:, 512:1024], BWk, 8)
            load_w(Wv, w_in[:, 1024:1536], BWv, 8)
            load_w(Wq, w_in[:, 0:512], BWq, 8)
            precast("g", w_in[:, 2048:4096], D, 2048)
            precast("na", w_na, 512, D)
            precast("fn", w_fn, 512, D)
            precast("out", w_out, D, D)
            precast("up", w_up, D, DFF)
            precast("dn", w_dn, DFF, D)
            NXS = 4
            xt = [P(sbt("a_xt%d" % i, [128, D], F32)) for i in range(NXS)]
            hb = [P(sbt("a_hb%d" % i, [128, D], BF16)) for i in range(NXS)]
            hT = [P(sbt("a_hT%d" % i, [128, 8, 512], BF16)) for i in range(2)]
            QT = [P(sbt("a_QT%d" % i, [128, 4, 512], BF16)) for i in range(2)]
            Kr = P(sbt("a_Kr", [128, 4, 16 * 128], BF16))
            Vr = P(sbt("a_Vr", [128, 16, NH, 65], BF16))
            KmT = P(sbt("a_KmT", [128, 4, 16], BF16))
            Vm = P(sbt("a_Vm", [16, NH, 65], BF16))
            EBg = P(sbt("a_EBg", [128, NH, 6, 128], BF16))
            EBs = [P(sbt("a_EBs%d" % i, [128, NH, 7, 128], BF16)) for i in range(2)]
            bst = [P(sbt("a_bst%d" % i, [128, 7 * 128], F32)) for i in range(2)]
            NE = 4
            SK = 3
            Et = [P(sbt("a_E%d" % i, [128, 7 * 128], BF16)) for i in range(NE)]
            Pt = [P(sbt("a_P%d" % i, [128, 7 * 128], BF16)) for i in range(NE)]
            osb = [P(sbt("a_o%d" % i, [128, 512], BF16)) for i in range(2)]
            rec = [P(sbt("a_rec%d" % i, [128, NH], F32)) for i in range(2)]
            oT = [P(sbt("a_oT%d" % i, [128, 4, 512], BF16)) for i in range(2)]
            Sps = P(pst("a_S", [128, 2048], F32))
            Opf = [P(pst("a_O%d" % i, [128, 512], F32)) for i in range(2)]
            Ops = [o[:, 0:260].rearrange("p (h e) -> p h e", e=65) for o in Opf]
            PXf = [P(pst("a_PX%d" % i, [128, 512], F32)) for i in range(1)]
            PXb = [P(pst("a_PT0", [128, 8, 128], BF16))]
            BPX = sy.bufs_n("aPX", 2)
            BPT = sy.bufs_n("aPT", 1)
            pxc = [0]
            Bxt = sy.bufs_n("axt", NXS)
            Bhb = sy.bufs_n("ahb", NXS)
            BhT, BQT, Bbst, Bosb, Brec, BoT, BS, BO, BEBs = (
                sy.bufs_n(n, 2) for n in ("ahT", "aQT", "abst", "aosb", "arec", "aoT", "aS", "aO", "aEBs"))
            BE = sy.bufs_n("aE", NE)
            BPt = sy.bufs_n("aP", NE)
            BK = sy.bufs_n("aK", 16)
            BV = sy.bufs_n("aV", 16)
            BKm, BVm, BEBg = (sy.buf(n) for n in ("aKm", "aVm", "aEBg"))
            BVones = sy.buf("aVones")
            sy.op("pool", lambda: G_.memset(Vr[:, :, :, 64:65], 1.0), writes=[BVones])
            sy.op("pool", lambda: G_.memset(Vm[:, :, 64:65], 1.0), writes=[BVones])
            for h in range(NH):
                s = h % 2
                sy.dma("sp", bst[s][:, 0:768], bgen[:, h, :, :].rearrange("p t q -> p (t q)"), writes=[Bbst[s]],
                       sem_of=Bbst[s])
                sy.op("act", lambda: A_.activation(out=EBg[:, h, :, :].rearrange("p t q -> p (t q)"),
                                                   in_=bst[s][:, 0:768], func=AF.Exp),
                      reads=[Bbst[s]], wacc=[BEBg])

            def xpose(h_sb, Bh, rows, dst_ap, Bdst, nchunk, wacc, eng=None):
                k = 0
                pT = PXb[k]

                def em():
                    i = None
                    for c in range(nchunk):
                        i = T_.transpose(out=pT[:, c, 0:rows], in_=h_sb[0:rows, c * 128:(c + 1) * 128],
                                         identity=ident[0:rows, 0:rows])
                    return i
                sy.op("pe", em, reads=[Bh, Bconst], writes=[BPT[k]])
                evac(dst_ap, pT[:, 0:nchunk, 0:rows], reads=[BPT[k]], wacc=[Bdst] if wacc else (),
                     writes=() if wacc else [Bdst], eng=eng)

            mmb = [(PXf[0], BPX[0]), (Opf[0], BO[0]), (Opf[1], BO[1])]

            def mm_group(nrow, ncol, lhs_fn, rhs_fn, reads):
                pm, Bpm_ = mmb[pxc[0] % 3]
                pxc[0] += 1

                def em():
                    i = None
                    for dc in range(8):
                        i = T_.matmul(pm[0:nrow, 0:ncol], lhsT=lhs_fn(dc), rhs=rhs_fn(dc), start=(dc == 0), stop=(dc == 7))
                    return i
                sy.op("pe", em, reads=reads, writes=[Bpm_])
                return pm, Bpm_

            sy.dma("sp", xt[0][0:16, :], meta[:, :], writes=[Bxt[0]], sem_of=Bxt[0])
            rmsnorm(xt[0][0:16, :], Bxt[0], 16, gbc_mix, hb[0][0:16, :], Bhb[0])
            xpose(hb[0], Bhb[0], 16, hT[0][:, :, 0:16], BhT[0], 8, False)
            for hp in range(4):
                pm, Bpm = mm_group(128, 16, lambda dc: Wk[:, dc, hp * 128:(hp + 1) * 128], lambda dc: hT[0][:, dc, 0:16],
                                   [BhT[0], BWk])
                evac(KmT[:, hp, :], pm[:, 0:16], reads=[Bpm], wacc=[BKm])
            pm, Bpm = mm_group(16, 512, lambda dc: hT[0][:, dc, 0:16], lambda dc: Wv[:, dc, :], [BhT[0], BWv])
            evac(Vm[0:16, :, 0:64], pm[0:16, :].rearrange("p (h d) -> p h d", d=64), reads=[Bpm, BVones], writes=[BVm])

            for g in (GP, GS):
                d = gd[g.name]
                xctr = [0]
                tslot = {}

                def ln_load(t):
                    s = xctr[0] % NXS
                    xctr[0] += 1
                    tslot[t] = s
                    sy.dma("sp", xt[s][:, :], d["xe"][128 * t:128 * t + 128, :], writes=[Bxt[s]], sem_of=Bxt[s])

                def ln_norm(t):
                    s = tslot[t]
                    rmsnorm(xt[s][:, :], Bxt[s], 128, gbc_mix, hb[s][:, :], Bhb[s])

                def ln(t):
                    ln_load(t)
                    ln_norm(t)

                def unit_tiles(u):
                    if u < 0:
                        return [0, 1]
                    if u < g.NB:
                        return [4 * u + 2 + i for i in range(4)]
                    if u == g.NB:
                        return [g.NP + 2, g.NP + 3]
                    return []

                def qkv_mm(u):
                    ts = unit_tiles(u)
                    if not ts:
                        return
                    with_q = 0 <= u < g.NB
                    hs = u % 2
                    ncols = 128 * len(ts)
                    for i, t in enumerate(ts):
                        s = tslot[t]
                        xpose(hb[s], Bhb[s], 128, hT[hs][:, :, i * 128:(i + 1) * 128], BhT[hs], 8, i > 0)
                    for hp in range(4):
                        pm, Bpm = mm_group(128, ncols, lambda dc: Wk[:, dc, hp * 128:(hp + 1) * 128],
                                           lambda dc: hT[hs][:, dc, 0:ncols], [BhT[hs], BWk])
                        e_ = "act" if hp % 2 == 0 else "dve"
                        for i, t in enumerate(ts):
                            sl = t % 16
                            evac(Kr[:, hp, sl * 128:(sl + 1) * 128], pm[:, i * 128:(i + 1) * 128], reads=[Bpm],
                                 wacc=[BK[sl]] if hp > 0 else (), writes=[BK[sl]] if hp == 0 else (), eng=e_)
                    if with_q:
                        for hp in range(4):
                            pm, Bpm = mm_group(128, ncols, lambda dc: Wq[:, dc, hp * 128:(hp + 1) * 128],
                                               lambda dc: hT[hs][:, dc, 0:ncols], [BhT[hs], BWq])
                            evac(QT[hs][:, hp, 0:ncols], pm[:, 0:ncols], reads=[Bpm],
                                 wacc=[BQT[hs]] if hp > 0 else (), writes=[BQT[hs]] if hp == 0 else ())
                    for i, t in enumerate(ts):
                        sl = t % 16
                        pm, Bpm = mm_group(128, 512, lambda dc: hT[hs][:, dc, i * 128:(i + 1) * 128],
                                           lambda dc: Wv[:, dc, :], [BhT[hs], BWv])
                        evac(Vr[:, sl, :, 0:64], pm[:, :].rearrange("p (h d) -> p h d", d=64), reads=[Bpm, BVones],
                             writes=[BV[sl]])
                    if with_q:
                        sy.dma("sp", d["hTd"][:, :, u * 512:(u + 1) * 512].rearrange("a p t -> p a t"), hT[hs][:],
                               reads=[BhT[hs]], wacc=[d["BhTd"]], sem_of=BhT[hs])

                specials = {p: (idx, t0, nt) for idx, (p, t0, nt) in enumerate(_special_pairs(g))}

                sp_q = []
                sp_staged = []
                sp_ctr = [0]

                def sp_enqueue(B):
                    nsp = 0
                    for p in range(4 * B, 4 * B + 4):
                        if p in specials:
                            for h in range(NH):
                                sp_q.append((p, nsp, h))
                            nsp += 1

                def sp_step(n=1):
                    for _ in range(n):
                        if sp_q:
                            p, slot, h = sp_q.pop(0)
                            idx, t0, nt = specials[p]
                            s_ = sp_ctr[0] % 2
                            sp_ctr[0] += 1
                            w_ = (nt + 1) * 128
                            sy.dma("sp", bst[s_][:, 0:w_], d["bs"][idx, :, h, 0:nt + 1, :].rearrange("p t q -> p (t q)"),
                                   writes=[Bbst[s_]], sem_of=Bbst[s_])
                            sp_staged.append((p, slot, h, s_))
                        if len(sp_staged) > 1 or (sp_staged and not sp_q):
                            p, slot, h, s_ = sp_staged.pop(0)
                            idx, t0, nt = specials[p]
                            w_ = (nt + 1) * 128
                            sy.op("act", lambda: A_.activation(out=EBs[slot][:, h, 0:nt + 1, :].rearrange("p t q -> p (t q)"),
                                                               in_=bst[s_][:, 0:w_], func=AF.Exp),
                                  reads=[Bbst[s_]], wacc=[BEBs[slot]] if h > 0 else (), writes=[BEBs[slot]] if h == 0 else ())

                def sp_flush():
                    while sp_q or sp_staged:
                        sp_step(1)

                def att_block(B, tiles_next):
                    pairs = list(range(4 * B, 4 * B + 4))
                    qs = B % 2
                    pinfo = {}
                    nsp = 0
                    for p in pairs:
                        if p in specials:
                            idx, t0, nt = specials[p]
                            pinfo[p] = (t0, nt, EBs[nsp], BEBs[nsp])
                            nsp += 1
                        else:
                            pinfo[p] = (p, 5, EBg, BEBg)
                    if B + 1 == g.NB - 1:
                        sp_enqueue(B + 1)
                    items = [(p, h) for p in pairs for h in range(NH)]
                    pend_T = []

                    def emit_S(k):
                        p, h = items[k]
                        t0, nt, EB, BEB = pinfo[p]
                        tiles = list(range(t0, t0 + nt))
                        qc = slice((p % 4) * 128, (p % 4 + 1) * 128)
                        s = k % 2
                        s3 = k % NE
                        se = k % NE
                        hp = h // 2
                        pr = (h % 2) * 64
                        S = Sps[:, s * 1024:s * 1024 + 896]
                        BSs = [BS[s]]
                        me = "pool" if k % 2 == 0 else "dve"
                        ME = G_ if k % 2 == 0 else V_

                        def em():
                            for i, t in enumerate(tiles):
                                sl = t % 16
                                T_.matmul(S[:, i * 128:(i + 1) * 128], lhsT=Kr[pr:pr + 64, hp, sl * 128:(sl + 1) * 128],
                                          rhs=QT[qs][pr:pr + 64, hp, qc], start=True, stop=True)
                            return T_.matmul(S[0:16, nt * 128:(nt + 1) * 128], lhsT=KmT[pr:pr + 64, hp, :],
                                             rhs=QT[qs][pr:pr + 64, hp, qc], start=True, stop=True)
                        sy.op("pe", em, reads=[BQT[qs], BKm] + [BK[t % 16] for t in tiles], writes=BSs)
                        sy.op("act", lambda: A_.activation(out=Et[se][:, 0:nt * 128], in_=S[:, 0:nt * 128], func=AF.Exp,
                                                           scale=DH ** -0.5), reads=BSs, writes=[BE[se]])
                        sy.op("act", lambda: A_.activation(out=Et[se][0:16, nt * 128:(nt + 1) * 128],
                                                           in_=S[0:16, nt * 128:(nt + 1) * 128], func=AF.Exp,
                                                           scale=DH ** -0.5), reads=BSs, wacc=[BE[se]])
                        sy.op(me, lambda: ME.tensor_tensor(out=Pt[s3][:, 0:nt * 128], in0=Et[se][:, 0:nt * 128],
                                                           in1=EB[:, h, 0:nt, :].rearrange("p t q -> p (t q)"),
                                                           op=ALU.mult), reads=[BE[se], BEB], writes=[BPt[s3]])
                        sy.op(me, lambda: ME.tensor_tensor(out=Pt[s3][0:16, nt * 128:(nt + 1) * 128],
                                                           in0=Et[se][0:16, nt * 128:(nt + 1) * 128],
                                                           in1=EB[0:16, h, nt, :], op=ALU.mult),
                              reads=[BE[se], BEB], wacc=[BPt[s3]])

                    def emit_PV(k):
                        p, h = items[k]
                        t0, nt, EB, BEB = pinfo[p]
                        tiles = list(range(t0, t0 + nt))
                        s3 = k % NE
                        O = Ops[h // 4]
                        hh = h % 4

                        def em():
                            for i, t in enumerate(tiles):
                                T_.matmul(O[:, hh, :], lhsT=Pt[s3][:, i * 128:(i + 1) * 128], rhs=Vr[:, t % 16, h, :],
                                          start=(i == 0), stop=False)
                            return T_.matmul(O[:, hh, :], lhsT=Pt[s3][0:16, nt * 128:(nt + 1) * 128], rhs=Vm[0:16, h, :],
                                             start=False, stop=True)
                        sy.op("pe", em, reads=[BPt[s3], BVm] + [BV[t % 16] for t in tiles],
                              wacc=[BO[h // 4]] if hh > 0 else (), writes=[BO[h // 4]] if hh == 0 else ())
                        if hh == 3:
                            a = h // 4
                            osl = p % 2
                            sy.op("dve", lambda: V_.reciprocal(out=rec[osl][:, a * 4:(a + 1) * 4].unsqueeze(2),
                                                               in_=Ops[a][:, :, 64:65]),
                                  reads=[BO[a]], wacc=[Brec[osl]] if a else (), writes=() if a else [Brec[osl]])
                            sy.op("dve", lambda: V_.tensor_tensor(
                                out=osb[osl][:, a * 256:(a + 1) * 256].rearrange("p (h d) -> p h d", d=64),
                                in0=Ops[a][:, :, 0:64],
                                in1=rec[osl][:, a * 4:(a + 1) * 4].unsqueeze(2).to_broadcast([128, 4, 64]),
                                op=ALU.mult), reads=[BO[a], Brec[osl]], wacc=[Bosb[osl]] if a else (),
                                writes=() if a else [Bosb[osl]])
                            if a == 1:
                                pend_T.append([p, 2])

                    def emit_T(p):
                        osl = p % 2
                        i4 = p % 4
                        qc = slice(i4 * 128, (i4 + 1) * 128)
                        xpose(osb[osl], Bosb[osl], 128, oT[qs][:, :, qc], BoT[qs], 4, i4 > 0)
                        if i4 == 3:
                            sy.dma("sp", d["oTd"][:, :, B * 512:(B + 1) * 512].rearrange("a p t -> p a t"), oT[qs][:],
                                   reads=[BoT[qs]], wacc=[d["BoTd"]], sem_of=BoT[qs])

                    n = len(items)
                    if tiles_next:
                        ln_load(tiles_next[0])
                    for k in range(n + SK):
                        if k % 2 == 0:
                            sp_step(1)
                        if k < n:
                            emit_S(k)
                            if items[k][1] == NH - 1:
                                i4 = items[k][0] % 4
                                if i4 < len(tiles_next):
                                    ln_norm(tiles_next[i4])
                                if i4 + 1 < len(tiles_next):
                                    ln_load(tiles_next[i4 + 1])
                                precast_some(2)
                        if k >= SK:
                            emit_PV(k - SK)
                        for e in list(pend_T):
                            if e[1] == 0:
                                emit_T(e[0])
                                pend_T.remove(e)
                            else:
                                e[1] -= 1
                    for e in pend_T:
                        emit_T(e[0])
                    sp_flush()

                sp_enqueue(0)
                for t in unit_tiles(-1):
                    sp_step(2)
                    ln(t)
                qkv_mm(-1)
                for t in unit_tiles(0):
                    sp_step(2)
                    ln(t)
                qkv_mm(0)
                for t in unit_tiles(1):
                    sp_step(2)
                    ln(t)
                qkv_mm(1)
                sp_flush()
                for B in range(g.NB):
                    att_block(B, unit_tiles(B + 2))
                    qkv_mm(B + 2)
            precast_some(len(precast_q))
            sy.barrier()

    def phase_A2():
        rms_mode[0] = "sqrt"
        with ExitStack() as ph:
            P = ph.enter_context
            Wg = P(sbt("Wg", [128, 8, 2048], BF16))
            Wna = P(sbt("Wna", [128, 4, D], BF16))
            Wfn = P(sbt("Wfn", [128, 4, D], BF16))
            Wf2 = P(sbt("Wf2", [128, 8, D], BF16))
            Wo = P(sbt("Wo", [128, 8, D], BF16))
            cs = P(sbt("cs_sb", [128, 2, 128], BF16))
            BWg, BWna, BWfn, BWf2, BWo, Bcs = (sy.buf(n) for n in ("Wg", "Wna", "Wfn", "Wf2", "Wo", "cs"))
            sy.dma("sp", cs[:], csd[:, :, :], writes=[Bcs], sem_of=Bcs)
            load_w(Wfn, wb["fn"][0], BWfn, 4, wb["fn"][1])
            load_w(Wg, wb["g"][0], BWg, 8, wb["g"][1])
            load_w(Wna, wb["na"][0], BWna, 4, wb["na"][1])
            load_w(Wo, wb["out"][0], BWo, 8, wb["out"][1])
            ps = [P(pst("b_ps%d" % i, [128, 512], F32)) for i in range(8)]
            Bps = sy.bufs_n("bps", 8)
            pctr = [0]

            def nextps():
                k = pctr[0] % 8
                pctr[0] += 1
                return ps[k], Bps[k]
            for cc in range(4):
                for ri in range(2):
                    for half in range(2):
                        p_, Bp_ = nextps()
                        sy.op("pe", lambda: T_.matmul(p_[:, :], lhsT=cs[:, ri, :], rhs=Wfn[:, cc, half * 512:(half + 1) * 512],
                                                      start=True, stop=True), reads=[Bcs, BWfn], writes=[Bp_])
                        evac(Wf2[:, cc * 2 + ri, half * 512:(half + 1) * 512], p_[:, :], reads=[Bp_], wacc=[BWf2])
            hT = [P(sbt("b_hT%d" % i, [128, 8, 512], BF16)) for i in range(2)]
            oT = [P(sbt("b_oT%d" % i, [128, 4, 512], BF16)) for i in range(2)]
            QT = [P(sbt("b_QT%d" % i, [128, 8, 512], BF16)) for i in range(2)]
            xr = [P(sbt("b_xr%d" % i, [128, D], F32)) for i in range(2)]
            sg = [P(sbt("b_sg%d" % i, [128, 512], F32)) for i in range(4)]
            tm = [P(sbt("b_tm%d" % i, [128, 512], F32)) for i in range(4)]
            mx = [P(sbt("b_mx%d" % i, [128, 8, 512], BF16)) for i in range(2)]
            x1 = [P(sbt("b_x1%d" % i, [128, D], F32)) for i in range(2)]
            BhT, BoT, BQT, Bxr, Bmx, Bx1 = (sy.bufs_n(n, 2) for n in ("bhT", "boT", "bQT", "bxr", "bmx", "bx1"))
            Bsg = sy.bufs_n("bsg", 4)
            Btm = sy.bufs_n("btm", 4)
            x1c = [0]
            for g in (GP, GS):
                d = gd[g.name]

                def loads(B):
                    s = B % 2
                    c = slice(B * 512, (B + 1) * 512)
                    sy.dma("sp", hT[s][:], d["hTd"][:, :, c].rearrange("a p t -> p a t"), reads=[d["BhTd"]],
                           writes=[BhT[s]], sem_of=BhT[s])
                    sy.dma("sp", oT[s][:], d["oTd"][:, :, c].rearrange("a p t -> p a t"), reads=[d["BoTd"]],
                           writes=[BoT[s]], sem_of=BoT[s])
                    sy.dma("sp", QT[s][:], d["Qd"][:, :, c].rearrange("a p t -> p a t"), reads=[d["BQd"]],
                           writes=[BQT[s]], sem_of=BQT[s])
                loads(0)
                for B in range(g.NB):
                    s = B % 2
                    if B + 1 < g.NB:
                        loads(B + 1)
                    for dm in range(8):
                        dsl = slice(dm * 128, (dm + 1) * 128)
                        k = dm % 2
                        for gi in range(2):
                            p_, Bp_ = nextps()

                            def em():
                                i = None
                                for dc in range(8):
                                    i = T_.matmul(p_[:, :], lhsT=Wg[:, dc, gi * 1024 + dm * 128:gi * 1024 + (dm + 1) * 128],
                                                  rhs=hT[s][:, dc, :], start=(dc == 0), stop=(dc == 7))
                                return i
                            sy.op("pe", em, reads=[BhT[s], BWg], writes=[Bp_])
                            sy.op("act", lambda: A_.activation(out=sg[k * 2 + gi][:, :], in_=p_[:, :], func=AF.Sigmoid),
                                  reads=[Bp_], writes=[Bsg[k * 2 + gi]])
                        p1, Bp1 = nextps()

                        def em():
                            i = None
                            for c4 in range(4):
                                i = T_.matmul(p1[:, :], lhsT=Wna[:, c4, dsl], rhs=oT[s][:, c4, :], start=(c4 == 0), stop=(c4 == 3))
                            return i
                        sy.op("pe", em, reads=[BoT[s], BWna], writes=[Bp1])
                        p2, Bp2 = nextps()

                        def em():
                            i = None
                            for c8 in range(8):
                                i = T_.matmul(p2[:, :], lhsT=Wf2[:, c8, dsl], rhs=QT[s][:, c8, :], start=(c8 == 0), stop=(c8 == 7))
                            return i
                        sy.op("pe", em, reads=[BQT[s], BWf2], writes=[Bp2])
                        sy.op("dve", lambda: V_.tensor_tensor(out=tm[k * 2][:, :], in0=p1[:, :], in1=sg[k * 2][:, :], op=ALU.mult),
                              reads=[Bp1, Bsg[k * 2]], writes=[Btm[k * 2]])
                        sy.op("dve", lambda: V_.tensor_tensor(out=tm[k * 2 + 1][:, :], in0=p2[:, :], in1=sg[k * 2 + 1][:, :], op=ALU.mult),
                              reads=[Bp2, Bsg[k * 2 + 1]], writes=[Btm[k * 2 + 1]])
                        sy.op("pool", lambda: G_.tensor_tensor(out=mx[s][:, dm, :], in0=tm[k * 2][:, :], in1=tm[k * 2 + 1][:, :], op=ALU.add),
                              reads=[Btm[k * 2], Btm[k * 2 + 1]], wacc=[Bmx[s]] if dm > 0 else (), writes=[Bmx[s]] if dm == 0 else ())
                    for tt in range(4):
                        xs_ = x1c[0] % 2
                        x1c[0] += 1
                        r0 = B * 512 + tt * 128
                        if B == 0 and tt == 0:
                            sy.dma("sp", xr[xs_][:, :], d["xe"][256 + r0:256 + r0 + 128, :], writes=[Bxr[xs_]], sem_of=Bxr[xs_])
                        r1 = r0 + 128
                        if r1 < g.Th:
                            xn_ = x1c[0] % 2
                            sy.dma("sp", xr[xn_][:, :], d["xe"][256 + r1:256 + r1 + 128, :], writes=[Bxr[xn_]], sem_of=Bxr[xn_])
                        for half in range(2):
                            p_, Bp_ = nextps()

                            def em():
                                i = None
                                for dm in range(8):
                                    i = T_.matmul(p_[:, :], lhsT=mx[s][:, dm, tt * 128:(tt + 1) * 128],
                                                  rhs=Wo[:, dm, half * 512:(half + 1) * 512], start=(dm == 0), stop=(dm == 7))
                                return i
                            sy.op("pe", em, reads=[Bmx[s], BWo], writes=[Bp_])
                            sy.op("dve", lambda: V_.tensor_tensor(out=x1[xs_][:, half * 512:(half + 1) * 512], in0=p_[:, :],
                                                                  in1=xr[xs_][:, half * 512:(half + 1) * 512], op=ALU.add),
                                  reads=[Bp_, Bxr[xs_]], wacc=[Bx1[xs_]] if half else (), writes=() if half else [Bx1[xs_]])
                        r0 = B * 512 + tt * 128
                        sy.dma("sp", d["x1d"][r0:r0 + 128, :], x1[xs_][:, :], reads=[Bx1[xs_]], wacc=[d["Bx1d"]], sem_of=Bx1[xs_])
            sy.barrier()

    def phase_B():
        TB = 256
        NT = TB // 128
        with ExitStack() as ph:
            P = ph.enter_context
            Wup = P(sbt("Wup", [128, 8, DFF], BF16))
            Wdn = P(sbt("Wdn", [128, 32, D], BF16))
            BWup, BWdn = sy.buf("Wup"), sy.buf("Wdn")
            load_w(Wup, wb["up"][0], BWup, 8, wb["up"][1])
            load_w(Wdn, wb["dn"][0], BWdn, 32, wb["dn"][1])
            ps = [P(pst("c_ps%d" % i, [128, 512], F32)) for i in range(6)]
            Bps = sy.bufs_n("cps", 6)
            pT = [P(pst("c_pT%d" % i, [128, 8, 128], BF16)) for i in range(2)]
            BpT = sy.bufs_n("cpT", 2)
            pctr = [0]

            def nextps():
                k = pctr[0] % 6
                pctr[0] += 1
                return ps[k], Bps[k]
            x1 = [P(sbt("c_x1%d" % i, [128, NT, D], F32)) for i in range(3)]
            hb = [P(sbt("c_hb%d" % i, [128, D], BF16)) for i in range(2)]
            hT = [P(sbt("c_hT%d" % i, [128, 8, TB], BF16)) for i in range(2)]
            rl = [P(sbt("c_rl%d" % i, [128, TB], BF16)) for i in range(2)]
            aT = P(sbt("c_aT", [128, 32, TB], BF16))
            x2 = [P(sbt("c_x2%d" % i, [128, D], F32)) for i in range(2)]
            Bhb, BhT, Brl, Bx2 = (sy.bufs_n(n, 2) for n in ("chb", "chT", "crl", "cx2"))
            Bx1 = sy.bufs_n("cx1", 3)
            BaT = sy.bufs_n("caT", 32)
            c2 = [0]
            for g in (GP, GS):
                d = gd[g.name]
                nblk = g.Th // TB

                def loads(B):
                    if B >= nblk:
                        return
                    s = B % 3
                    sy.dma("sp", x1[s][:], d["x1d"][B * TB:(B + 1) * TB, :].rearrange("(t p) f -> p t f", p=128),
                           reads=[d["Bx1d"]], writes=[Bx1[s]], sem_of=Bx1[s])

                def stage_N(B):
                    if B >= nblk:
                        return
                    for tt in range(NT):
                        rmsnorm(x1[B % 3][:, tt, :], Bx1[B % 3], 128, gbc_mlp, hb[tt][:, :], Bhb[tt])

                def stage_X(B):
                    if B >= nblk:
                        return
                    for tt in range(NT):
                        transpose_to(pT[tt], BpT[tt], hb[tt], Bhb[tt], 128, lambda: hT[B % 2][:, :, tt * 128:(tt + 1) * 128],
                                     BhT[B % 2], wacc=(tt > 0))

                def stage_up(B):
                    s = B % 2
                    for fc in range(32):
                        p_, Bp_ = nextps()

                        def em():
                            i = None
                            for dc in range(8):
                                i = T_.matmul(p_[:, 0:TB], lhsT=Wup[:, dc, fc * 128:(fc + 1) * 128], rhs=hT[s][:, dc, :],
                                              start=(dc == 0), stop=(dc == 7))
                            return i
                        sy.op("pe", em, reads=[BhT[s], BWup], writes=[Bp_])
                        k = fc % 2
                        sy.op("act", lambda: A_.activation(out=rl[k][:, :], in_=p_[:, 0:TB], func=AF.Relu),
                              reads=[Bp_], writes=[Brl[k]])
                        sy.op("pool", lambda: G_.tensor_tensor(out=aT[:, fc, :], in0=rl[k][:, :], in1=rl[k][:, :], op=ALU.mult),
                              reads=[Brl[k]], writes=[BaT[fc]])
                        if fc == 7:
                            stage_final(B - 1)
                            stage_N(B + 1)

                pend_final = {}

                def stage_final(B):
                    for (xs_, r0) in pend_final.pop(B, []):
                        rmsnorm(x2[xs_][:, :], Bx2[xs_], 128, gbc_fin, x2[xs_][:, :], Bx2[xs_])
                        sy.dma("sp", d["y"][r0:r0 + 128, :], x2[xs_][:, :], reads=[Bx2[xs_]], wacc=[d["By"]], sem_of=Bx2[xs_])

                def stage_down(B):
                    s = B % 3
                    for tt in range(NT):
                        xs_ = c2[0] % 2
                        c2[0] += 1
                        for half in range(2):
                            p_, Bp_ = nextps()

                            def em():
                                i = None
                                for fc in range(32):
                                    i = T_.matmul(p_[:, :], lhsT=aT[:, fc, tt * 128:(tt + 1) * 128],
                                                  rhs=Wdn[:, fc, half * 512:(half + 1) * 512], start=(fc == 0), stop=(fc == 31))
                                return i
                            sy.op("pe", em, reads=BaT + [BWdn], writes=[Bp_])
                            sy.op("dve", lambda: V_.tensor_tensor(out=x2[xs_][:, half * 512:(half + 1) * 512], in0=p_[:, :],
                                                                  in1=x1[s][:, tt, half * 512:(half + 1) * 512], op=ALU.add),
                                  reads=[Bp_, Bx1[s]], wacc=[Bx2[xs_]] if half else (), writes=() if half else [Bx2[xs_]])
                        pend_final.setdefault(B, []).append((xs_, B * TB + tt * 128))

                loads(0)
                loads(1)
                stage_N(0)
                stage_X(0)
                for B in range(nblk):
                    stage_up(B)
                    stage_X(B + 1)
                    stage_down(B)
                    loads(B + 2)
                stage_final(nblk - 1)
            sy.barrier()

    done = False
    for g in (GP, GS):
        if phase_F(g):
            done = True
            break
    if not done and stop_after != "F":
        phase_A1()
        if stop_after != "A1":
            phase_A2()
            if stop_after != "A2":
                phase_B()
    sy.barrier(engines=["sp"])
    top.close()
    return nc


_NC_CACHE = {}


def _prep_inputs(x_prompt, x_sample, meta_tokens, w_in, rel_bias, meta_bias, w_branch_na, w_branch_fn,
                 w_out, g_mix, g_mlp, w_up, w_down, g_final):
    f = lambda a: np.ascontiguousarray(np.asarray(a, dtype=np.float32))
    ident, w16, cs = _host_consts()
    rb = f(rel_bias)[0]
    mb = f(meta_bias)[0]
    meta = f(meta_tokens)
    common = dict(
        w_in=f(w_in)[0], w_na=f(w_branch_na)[0], w_fn=f(w_branch_fn)[0], w_out=f(w_out)[0], w_up=f(w_up)[0],
        w_dn=f(w_down)[0], g_mix=f(g_mix).reshape(1, D), g_mlp=f(g_mlp).reshape(1, D), g_fin=f(g_final).reshape(1, D),
        meta=meta, ident=ident, w16=w16, cs=cs,
        bgen=_bias_tiles(rb, mb, GP, 0, 16, 12, 5),
    )
    xs = dict(p=f(x_prompt), s=f(x_sample))
    Gc = {}
    in_maps = []
    for c in range(8):
        b, hf = c // 2, c % 2
        m = dict(common)
        for g in (GP, GS):
            n = g.name
            x = xs[n][b]
            m["xs_" + n] = np.concatenate([meta, x], 0)
            xe = np.zeros((g.next, D), np.float32)
            lo = (g.R * hf - 4) * GW
            hi = lo + g.next
            a, bnd = max(lo, 0), min(hi, g.T)
            xe[a - lo:bnd - lo] = x[a:bnd]
            m["xe_" + n] = xe
            if (n, hf) not in Gc:
                Gc[(n, hf)] = _host_G(g, hf) + (np.stack([
                    np.pad(_bias_tiles(rb, mb, g, hf, 2 * p, 2 * t0 - 4, nt), ((0, 0), (0, 0), (0, 6 - nt), (0, 0)),
                           constant_values=NEG) for (p, t0, nt) in _special_pairs(g)]),)
            m["gm_" + n], m["gl_" + n], m["bs_" + n] = Gc[(n, hf)]
        in_maps.append(m)
    return in_maps


def kernel(x_prompt, x_sample, meta_tokens, w_in, rel_bias, meta_bias, w_branch_na, w_branch_fn,
           w_out, g_mix, g_mlp, w_up, w_down, g_final):
    in_maps = _prep_inputs(x_prompt, x_sample, meta_tokens, w_in, rel_bias, meta_bias, w_branch_na, w_branch_fn,
                           w_out, g_mix, g_mlp, w_up, w_down, g_final)
    if "nc" not in _NC_CACHE:
        _NC_CACHE["nc"] = build_nc()
    res = run_bass_kernel_spmd(_NC_CACHE["nc"], in_maps, core_ids=list(range(8)))
    yp = np.zeros((4, GP.T, D), np.float32)
    ys = np.zeros((4, GS.T, D), np.float32)
    for c in range(8):
        b, hf = c // 2, c % 2
        r = res.results[c]
        yp[b, hf * GP.Th:(hf + 1) * GP.Th] = r["y_p"]
        ys[b, hf * GS.Th:(hf + 1) * GS.Th] = r["y_s"]
    return (yp, ys)
```

```python
import numpy as np
import ml_dtypes
from contextlib import ExitStack
import concourse.bass as bass
import concourse.mybir as mybir
from concourse.bass_utils import run_bass_kernel_spmd

F32 = mybir.dt.float32
BF16 = mybir.dt.bfloat16
AF = mybir.ActivationFunctionType
ALU = mybir.AluOpType
NPBF = ml_dtypes.bfloat16

D = 1024
NMETA = 16
GW = 64
NH = 8
DH = 64
DFF = 4096
EPS = 1e-6
NEG = -30000.0
S1 = 0.25


class Grp:
    def __init__(self, name, T):
        self.name = name
        self.T = T
        self.N = T + NMETA
        self.M = self.N // 16
        self.rows_total = T // GW
        self.R = self.rows_total // 2
        self.Th = self.R * GW
        self.NP = self.R // 2
        self.NB = self.Th // 512
        self.blk = (self.M - 1) // 8
        self.nch = (self.M - 1) // 128
        self.nj = self.Th // 16
        self.ncol = 2 * self.nj
        self.next = (self.R + 8) * GW
        self.nkv = self.NP + 4


GP = Grp("p", 8192)
GS = Grp("s", 4096)


def _bias_tiles(rel_bias, meta_bias, g, hf, lr, krow0, nt):
    out = np.full((128, NH, nt + 1, 128), NEG, np.float32)
    q = np.arange(128)
    qr = g.R * hf + lr + q // 64
    qc = q % 64
    rs = np.clip(qr - 4, 0, g.rows_total - 8)
    cs = np.clip(qc - 8, 0, GW - 16)
    k = np.arange(128)
    for i in range(nt):
        kr = g.R * hf + krow0 + 2 * i + k // 64
        kc = k % 64
        valid = ((kr[:, None] >= 0) & (kr[:, None] < g.rows_total)
                 & (kr[:, None] >= rs[None, :]) & (kr[:, None] < rs[None, :] + 8)
                 & (kc[:, None] >= cs[None, :]) & (kc[:, None] < cs[None, :] + 16))
        dr = np.clip(kr[:, None] - qr[None, :] + 7, 0, 14)
        dc = np.clip(kc[:, None] - qc[None, :] + 15, 0, 30)
        vals = rel_bias[:, dr, dc]
        out[:, :, i, :] = np.where(valid[None], vals, np.float32(NEG)).transpose(1, 0, 2)
    out[:NMETA, :, nt, :] = np.broadcast_to(meta_bias.T[:, :, None], (NMETA, NH, 128))
    return out


def _special_pairs(g):
    return [(0, 0, 6), (1, 1, 5), (g.NP - 2, g.NP - 2, 5), (g.NP - 1, g.NP - 2, 6)]


def _host_consts():
    ident = np.eye(128, dtype=np.float32).astype(NPBF)
    n1 = np.arange(16)
    ang = 2 * np.pi * np.outer(n1, n1) / 16.0
    w16 = np.zeros((128, 256), np.float32)
    for j in range(8):
        w16[j * 16:(j + 1) * 16, j * 32:j * 32 + 16] = np.cos(ang) * S1
        w16[j * 16:(j + 1) * 16, j * 32 + 16:j * 32 + 32] = -np.sin(ang) * S1
    c = np.arange(128)
    a2 = 2 * np.pi * np.outer(c, c) / 128.0
    cs = np.stack([np.cos(a2), np.sin(a2)], 1) / np.sqrt(128.0)
    return ident, w16.astype(NPBF), cs.astype(np.float32).astype(NPBF)


def _host_G(g, hf):
    sc = (1.0 / S1) / np.sqrt(float(g.N))
    k1 = np.arange(16)[:, None, None]
    n2 = np.arange(g.M)[None, :, None]
    k2 = (1 + hf * g.nj + np.arange(g.nj))[None, None, :]
    ph = (n2 * (k1 + 16 * k2)) % g.N
    ang = 2 * np.pi * ph.astype(np.float64) / g.N
    Gr = np.cos(ang) * sc
    Gi = -np.sin(ang) * sc
    forAr = np.concatenate([Gr, Gi], -1)
    forAi = np.concatenate([-Gi, Gr], -1)
    both = np.stack([forAr, forAi], 2)
    main = both[:, :g.M - 1].reshape(16, g.nch, 128, 2, g.ncol).transpose(0, 2, 1, 3, 4)
    last = both[:, g.M - 1]
    return (np.ascontiguousarray(main).astype(np.float32).astype(NPBF),
            np.ascontiguousarray(last).astype(np.float32).astype(NPBF))


class Buf:
    __slots__ = ("name", "w", "r", "pw", "pr", "sem", "semv")

    def __init__(self, name):
        self.name = name
        self.w = {}
        self.r = {}
        self.pw = {}
        self.pr = {}
        self.sem = None
        self.semv = 0


class Sy:
    ENG = ("pe", "act", "dve", "pool", "sp")

    def __init__(self, nc, es):
        self.nc = nc
        self.es = es
        self.eng = dict(pe=nc.tensor, act=nc.scalar, dve=nc.vector, pool=nc.gpsimd, sp=nc.sync)
        self.csem = {}
        self.ccnt = {}
        self.waited = {}
        self.nsem = 0
        self.bufs = []
        for e in self.ENG:
            self.csem[e] = self.newsem("c_" + e)
            self.ccnt[e] = 0

    def newsem(self, name):
        self.nsem += 1
        return self.es.enter_context(self.nc.semaphore("%s_%d" % (name, self.nsem)))

    def buf(self, name):
        b = Buf(name)
        self.bufs.append(b)
        return b

    def bufs_n(self, name, n):
        return [self.buf("%s%d" % (name, i)) for i in range(n)]

    def _wait(self, e, evs):
        best = {}
        for ev in evs:
            k = id(ev[0])
            if k not in best or best[k][1] < ev[1]:
                best[k] = ev
        for sem, val, src in best.values():
            if src == e:
                if e == "pe":
                    continue
                if sem is not self.csem[e]:
                    continue
                if self.ccnt[e] - val >= 2:
                    continue
            if self.waited.get((e, id(sem)), 0) >= val:
                continue
            self.eng[e].wait_ge(sem, val)
            self.waited[(e, id(sem))] = val

    def _deps(self, reads, writes, wacc):
        evs = []
        for b in reads:
            evs.extend(b.w.values())
        for b in writes:
            evs.extend(b.w.values())
            evs.extend(b.r.values())
        for b in wacc:
            evs.extend(b.r.values())
            evs.extend(b.pr.values())
            evs.extend(b.pw.values())
        return evs

    @staticmethod
    def _put(d, ev):
        k = id(ev[0])
        if k not in d or d[k][1] < ev[1]:
            d[k] = ev

    def _reg(self, ev, reads, writes, wacc):
        for b in reads:
            self._put(b.r, ev)
        for b in writes:
            b.pw = b.w
            b.pr = b.r
            b.w = {id(ev[0]): ev}
            b.r = {}
        for b in wacc:
            if b.r:
                b.pw = b.w
                b.pr = b.r
                b.w = {}
                b.r = {}
            self._put(b.w, ev)

    def op(self, e, emit, reads=(), writes=(), wacc=()):
        self._wait(e, self._deps(reads, writes, wacc))
        inst = emit()
        if self.ccnt[e] >= 30000:
            self.csem[e] = self.newsem("c_" + e)
            self.ccnt[e] = 0
        self.ccnt[e] += 1
        inst.then_inc(self.csem[e], 1)
        ev = (self.csem[e], self.ccnt[e], e)
        self._reg(ev, reads, writes, wacc)
        return ev

    def dma(self, q, out, in_, reads=(), writes=(), wacc=(), sem_of=None):
        self._wait(q, self._deps(reads, writes, wacc))
        inst = self.eng[q].dma_start(out=out, in_=in_)
        b = sem_of
        if b.sem is None:
            b.sem = self.newsem("d_" + b.name)
        b.semv += 16
        assert b.semv < 60000
        inst.then_inc(b.sem, 16)
        ev = (b.sem, b.semv, None)
        self._reg(ev, reads, writes, wacc)
        return ev

    def barrier(self, engines=None):
        evs = []
        for b in self.bufs:
            evs.extend(b.w.values())
            evs.extend(b.r.values())
            evs.extend(b.pw.values())
            evs.extend(b.pr.values())
        for e in (engines or self.ENG):
            best = {}
            for ev in evs:
                k = id(ev[0])
                if k not in best or best[k][1] < ev[1]:
                    best[k] = ev
            for sem, val, src in best.values():
                if src == e and e == "pe":
                    continue
                if self.waited.get((e, id(sem)), 0) >= val:
                    continue
                self.eng[e].wait_ge(sem, val)
                self.waited[(e, id(sem))] = val


def build_nc(debug=None):
    nc = bass.Bass("TRN2", target_bir_lowering=False)
    top = ExitStack()
    sy = Sy(nc, top)
    T_ = nc.tensor
    A_ = nc.scalar
    V_ = nc.vector
    G_ = nc.gpsimd

    uid = [0]

    def sbt(name, shape, dt):
        uid[0] += 1
        return nc.sbuf_tensor("%s_u%d" % (name, uid[0]), shape, dt)

    def pst(name, shape, dt):
        uid[0] += 1
        return nc.psum_tensor("%s_u%d" % (name, uid[0]), shape, dt)

    def din(name, shape, dt=F32):
        return nc.dram_tensor(name, list(shape), dt, kind="ExternalInput").ap()

    def dscr(name, shape, dt):
        kind = "ExternalOutput" if (debug and name in debug) else "Internal"
        return nc.dram_tensor(name, list(shape), dt, kind=kind).ap()

    w_in = din("w_in", [D, 4096])
    w_na = din("w_na", [512, D])
    w_fn = din("w_fn", [512, D])
    w_out = din("w_out", [D, D])
    w_up = din("w_up", [D, DFF])
    w_dn = din("w_dn", [DFF, D])
    g_mix = din("g_mix", [1, D])
    g_mlp = din("g_mlp", [1, D])
    g_fin = din("g_fin", [1, D])
    meta = din("meta", [NMETA, D])
    identd = din("ident", [128, 128], BF16)
    w16d = din("w16", [128, 256], BF16)
    csd = din("cs", [128, 2, 128], BF16)
    bgen = din("bgen", [128, NH, 6, 128])
    gd = {}
    for g in (GP, GS):
        n = g.name
        gd[n] = dict(
            xs=din("xs_" + n, [g.N, D]),
            xe=din("xe_" + n, [g.next, D]),
            gm=din("gm_" + n, [16, 128, g.nch, 2, g.ncol], BF16),
            gl=din("gl_" + n, [16, 2, g.ncol], BF16),
            bs=din("bs_" + n, [4, 128, NH, 7, 128]),
            y=nc.dram_tensor("y_" + n, [g.Th, D], F32, kind="ExternalOutput").ap(),
            Ud=dscr("Ud_" + n, [g.N, 512], BF16),
            Ad=dscr("Ad_" + n, [2, 16, g.M, 512], BF16),
            Qd=dscr("Qd_" + n, [8, 128, g.Th], BF16),
            hTd=dscr("hTd_" + n, [8, 128, g.Th], BF16),
            oTd=dscr("oTd_" + n, [4, 128, g.Th], BF16),
            x1d=dscr("x1d_" + n, [g.Th, D], F32),
        )
        for k in ("Ud", "Ad", "Qd", "hTd", "oTd", "x1d", "y"):
            gd[n]["B" + k] = sy.buf(k + "_" + n)

    stop_after = (debug or {}).get("stop") if isinstance(debug, dict) else None

    E = top.enter_context
    ident = E(sbt("ident_sb", [128, 128], BF16))
    gbc_mix = E(sbt("gbc_mix", [128, D], F32))
    gbc_mlp = E(sbt("gbc_mlp", [128, D], F32))
    gbc_fin = E(sbt("gbc_fin", [128, D], F32))
    Bconst = sy.buf("const")
    sy.dma("sp", ident[:], identd[:, :], writes=[Bconst], sem_of=Bconst)
    sy.dma("sp", gbc_mix[:], g_mix[0:1, :].partition_broadcast(128), wacc=[Bconst], sem_of=Bconst)
    sy.dma("sp", gbc_mlp[:], g_mlp[0:1, :].partition_broadcast(128), wacc=[Bconst], sem_of=Bconst)
    sy.dma("sp", gbc_fin[:], g_fin[0:1, :].partition_broadcast(128), wacc=[Bconst], sem_of=Bconst)

    def load_w(dst, src_ap, B, nsplit, Bsrc=None):
        kcs = dst.shape[1]
        for kc in range(kcs):
            if Bsrc is None:
                sy.dma("pool", dst[:, kc, :], src_ap[kc * 128:(kc + 1) * 128, :], wacc=[B], sem_of=B)
            else:
                sy.dma("sp", dst[:, kc, :], src_ap[kc * 128:(kc + 1) * 128, :], reads=[Bsrc], wacc=[B], sem_of=B)

    wb = {}

    precast_q = []

    def precast(name, src_ap, rows, cols):
        dst = dscr("wb_" + name, [rows, cols], BF16)
        B = sy.buf("wb_" + name)
        for r in range(0, rows, 128):
            precast_q.append((dst[r:r + 128, :], src_ap[r:r + 128, :], B))
        wb[name] = (dst, B)

    def precast_some(n):
        for _ in range(n):
            if not precast_q:
                return
            o_, i_, B = precast_q.pop(0)
            sy.dma("pool", o_, i_, wacc=[B], sem_of=B)

    ss = [E(sbt("ss%d" % i, [128, 1], F32)) for i in range(2)]
    sd = [E(sbt("sd%d" % i, [128, 1], F32)) for i in range(2)]
    rstd = [E(sbt("rstd%d" % i, [128, 1], F32)) for i in range(2)]
    junk = E(sbt("junk", [128, D], BF16))
    Bss = sy.bufs_n("ss", 2)
    Bsd = sy.bufs_n("sd", 2)
    Brs = sy.bufs_n("rstd", 2)
    Bjunk = sy.buf("junk")
    rms_ctr = [0]
    rms_mode = ["sqrt"]

    def rmsnorm(x_ap, Bx, rows, gbc, out_ap, Bout):
        s = rms_ctr[0] % 2
        rms_ctr[0] += 1
        sy.op("act", lambda: A_.activation(out=junk[0:rows, :], in_=x_ap, func=AF.Square,
                                           accum_out=ss[s][0:rows, :]),
              reads=[Bx], writes=[Bss[s], Bjunk])
        if rms_mode[0] == "lnexp":
            sy.op("act", lambda: A_.activation(out=sd[s][0:rows, :], in_=ss[s][0:rows, :], func=AF.Ln,
                                               scale=1.0 / D, bias=EPS),
                  reads=[Bss[s]], writes=[Bsd[s]])
            sy.op("act", lambda: A_.activation(out=rstd[s][0:rows, :], in_=sd[s][0:rows, :], func=AF.Exp, scale=-0.5),
                  reads=[Bsd[s]], writes=[Brs[s]])
        else:
            sy.op("act", lambda: A_.activation(out=sd[s][0:rows, :], in_=ss[s][0:rows, :], func=AF.Sqrt,
                                               scale=1.0 / D, bias=EPS),
                  reads=[Bss[s]], writes=[Bsd[s]])
            sy.op("dve", lambda: V_.reciprocal(out=rstd[s][0:rows, :], in_=sd[s][0:rows, :]),
                  reads=[Bsd[s]], writes=[Brs[s]])
        sy.op("dve", lambda: V_.scalar_tensor_tensor(out=out_ap, in0=x_ap, scalar=rstd[s][0:rows, 0:1],
                                                     in1=gbc[0:rows, :], op0=ALU.mult, op1=ALU.mult),
              reads=[Bx, Brs[s], Bconst], writes=[Bout])

    cp_ctr = [0]

    def evac(out_ap, in_ap, reads, writes=(), wacc=(), eng=None):
        if eng is None:
            eng = "act" if cp_ctr[0] % 2 == 0 else "dve"
            cp_ctr[0] += 1
        if eng == "act":
            return sy.op("act", lambda: A_.copy(out=out_ap, in_=in_ap), reads=reads, writes=writes, wacc=wacc)
        return sy.op("dve", lambda: V_.tensor_copy(out=out_ap, in_=in_ap), reads=reads, writes=writes, wacc=wacc)

    def transpose_to(pT, BpT, h_sb, Bh, rows, dst_ap_fn, Bdst, nchunk=8, wacc=True, eng=None):
        def em():
            i = None
            for c in range(nchunk):
                i = T_.transpose(out=pT[:, c, 0:rows], in_=h_sb[0:rows, c * 128:(c + 1) * 128],
                                 identity=ident[0:rows, 0:rows])
            return i
        sy.op("pe", em, reads=[Bh, Bconst], writes=[BpT])
        evac(dst_ap_fn(), pT[:, 0:nchunk, 0:rows], reads=[BpT], wacc=[Bdst] if wacc else (), writes=() if wacc else [Bdst],
             eng=eng)

    def phase_F(g):
        d = gd[g.name]
        with ExitStack() as ph:
            P = ph.enter_context
            Wu = P(sbt("Wu", [128, 8, 512], BF16))
            BWu = sy.buf("Wu")
            load_w(Wu, w_in[:, 1536:2048], BWu, 8)
            NXF = 6
            xt = [P(sbt("f_xt%d" % i, [128, D], F32)) for i in range(NXF)]
            hb = [P(sbt("f_hb%d" % i, [128, D], BF16)) for i in range(3)]
            hT = [P(sbt("f_hT%d" % i, [128, 8, 128], BF16)) for i in range(2)]
            Us = [P(sbt("f_Us%d" % i, [128, 512], BF16)) for i in range(3)]
            pT = [P(pst("f_pT%d" % i, [128, 8, 128], BF16)) for i in range(2)]
            pU = [P(pst("f_pU%d" % i, [128, 512], F32)) for i in range(6)]
            Bxt = sy.bufs_n("fxt", NXF)
            Bhb, BhT, BUs, BpT, BpU = (sy.bufs_n(n, k) for n, k in
                                       (("fhb", 3), ("fhT", 2), ("fUs", 3), ("fpT", 2), ("fpU", 6)))
            w16 = P(sbt("w16", [128, 256], BF16))
            Bw16 = sy.buf("w16")
            sy.dma("sp", w16[:], w16d[:, :], writes=[Bw16], sem_of=Bw16)
            SB = 8
            Ast = [P(sbt("f_Ast%d" % i, [128, 2, SB, 512], BF16)) for i in range(2)]
            BAst = sy.bufs_n("fAst", 2)
            xs3 = d["xs"].rearrange("(a m) f -> a m f", a=16)
            Adv = d["Ad"].rearrange("r k m c -> (r k) m c")
            ntile = g.blk + 1
            puc = [0]

            def nextpu():
                k = puc[0] % 6
                puc[0] += 1
                return pU[k], BpU[k]

            def rows_of(t):
                return 128 if t < g.blk else 16

            def f1_L(t):
                if t >= ntile:
                    return
                s = t % NXF
                if t < g.blk:
                    for j in range(8):
                        sy.dma("sp", xt[s][j * 16:(j + 1) * 16, :], xs3[:, j * g.blk + t, :],
                               wacc=[Bxt[s]] if j else (), writes=() if j else [Bxt[s]], sem_of=Bxt[s])
                else:
                    sy.dma("sp", xt[s][0:16, :], xs3[:, g.M - 1, :], writes=[Bxt[s]], sem_of=Bxt[s])

            def f1_N(t):
                if t >= ntile:
                    return
                rows = rows_of(t)
                rmsnorm(xt[t % NXF][0:rows, :], Bxt[t % NXF], rows, gbc_mix, hb[t % 3][0:rows, :], Bhb[t % 3])

            def f1_X(t):
                if t >= ntile:
                    return
                rows = rows_of(t)
                transpose_to(pT[t % 2], BpT[t % 2], hb[t % 3], Bhb[t % 3], rows, lambda: hT[t % 2][:, :, 0:rows], BhT[t % 2],
                             wacc=False, eng="dve")

            def f1_M(t):
                rows = rows_of(t)
                s = t % 2
                pu, Bpu = nextpu()

                def em():
                    i = None
                    for dc in range(8):
                        i = T_.matmul(pu[0:rows, :], lhsT=hT[s][:, dc, 0:rows], rhs=Wu[:, dc, :],
                                      start=(dc == 0), stop=(dc == 7))
                    return i
                sy.op("pe", em, reads=[BhT[s], BWu], writes=[Bpu])
                evac(Us[t % 3][0:rows, :], pu[0:rows, :], reads=[Bpu], writes=[BUs[t % 3]], eng="act")

            def f1_S(t):
                a_s = (t // SB) % 2
                i = t % SB
                if t < g.blk:
                    for half in range(2):
                        pu, Bpu = nextpu()
                        sy.op("pe", lambda: T_.matmul(pu[:, :], lhsT=w16[:, half * 128:(half + 1) * 128],
                                                      rhs=Us[t % 3][:, :], start=True, stop=True),
                              reads=[BUs[t % 3], Bw16], writes=[Bpu])
                        evac(Ast[a_s][:, half, i, :], pu[:, :], reads=[Bpu], wacc=[BAst[a_s]],
                             eng="act" if half == 0 else "dve")
                    if i == SB - 1:
                        sc = t // SB
                        for half in range(2):
                            for jj in range(4):
                                n2a = (half * 4 + jj) * g.blk + sc * SB
                                sy.dma("pool", Adv[:, n2a:n2a + SB, :], Ast[a_s][jj * 32:(jj + 1) * 32, half, :, :],
                                       reads=[BAst[a_s]], wacc=[d["BAd"]], sem_of=BAst[a_s])
                else:
                    pu, Bpu = nextpu()
                    sy.op("pe", lambda: T_.matmul(pu[0:32, :], lhsT=w16[0:16, 0:32], rhs=Us[t % 3][0:16, :],
                                                  start=True, stop=True), reads=[BUs[t % 3], Bw16], writes=[Bpu])
                    evac(Ast[a_s][0:32, 0, 0, :], pu[0:32, :], reads=[Bpu], writes=[BAst[a_s]], eng="act")
                    sy.dma("pool", Adv[:, g.M - 1, :], Ast[a_s][0:32, 0, 0, :], reads=[BAst[a_s]], wacc=[d["BAd"]],
                           sem_of=BAst[a_s])
            PF = NXF - 1
            for t in range(PF):
                f1_L(t)
            f1_N(0)
            f1_N(1)
            f1_X(0)
            for t in range(ntile + 1):
                f1_L(t + PF)
                f1_N(t + 2)
                f1_X(t + 1)
                if t < ntile:
                    f1_M(t)
                if t >= 1:
                    f1_S(t - 1)
            if stop_after in ("F1" + g.name, "F2" + g.name):
                return True
            nch, ncol, nj = g.nch, g.ncol, g.nj
            Ak = [P(sbt("f_Ak%d" % i, [128, nch, 2, 512], BF16)) for i in range(2)]
            Akl = [P(sbt("f_Akl%d" % i, [1, 2, 512], BF16)) for i in range(2)]
            Gk = [P(sbt("f_Gk%d" % i, [128, nch, 2, ncol], BF16)) for i in range(2)]
            Gkl = [P(sbt("f_Gkl%d" % i, [1, 2, ncol], BF16)) for i in range(2)]
            Qst = P(sbt("f_Qst", [128, 8, g.Th], BF16))
            BAk = sy.bufs_n("fAk", 2)
            BGk = sy.bufs_n("fGk", 2)
            BQst = sy.buf("fQst")
            Qv = Qst[:].rearrange("p a (j k) -> p a j k", k=16)
            def f3_load(k1):
                if k1 >= 16:
                    return
                s = k1 % 2
                sy.dma("sp", Gk[s][:], d["gm"][k1], writes=[BGk[s]], sem_of=BGk[s])
                sy.dma("sp", Gkl[s][0:1, :, :], d["gl"][k1:k1 + 1, :, :], wacc=[BGk[s]], sem_of=BGk[s])
                for ri in range(2):
                    sy.dma("sp", Ak[s][:, :, ri, :],
                           d["Ad"][ri, k1, 0:g.M - 1, :].rearrange("(ch p) c -> p ch c", p=128),
                           reads=[d["BAd"]], wacc=[BAk[s]], sem_of=BAk[s])
                sy.dma("sp", Akl[s][0:1, :, :], d["Ad"][:, k1, g.M - 1:g.M, :].rearrange("r o c -> o r c"),
                       reads=[d["BAd"]], wacc=[BAk[s]], sem_of=BAk[s])
            f3_load(0)
            for k1 in range(16):
                s = k1 % 2
                f3_load(k1 + 1)
                for cc in range(4):
                    k = (k1 * 4 + cc) % 4
                    pu = pU[k]

                    def em():
                        first = True
                        for ch in range(nch):
                            for ri in range(2):
                                T_.matmul(pu[:, 0:ncol], lhsT=Ak[s][:, ch, ri, cc * 128:(cc + 1) * 128],
                                          rhs=Gk[s][:, ch, ri, :], start=first, stop=False)
                                first = False
                        T_.matmul(pu[:, 0:ncol], lhsT=Akl[s][0:1, 0, cc * 128:(cc + 1) * 128],
                                  rhs=Gkl[s][0:1, 0, :], start=False, stop=False)
                        return T_.matmul(pu[:, 0:ncol], lhsT=Akl[s][0:1, 1, cc * 128:(cc + 1) * 128],
                                         rhs=Gkl[s][0:1, 1, :], start=False, stop=True)
                    sy.op("pe", em, reads=[BAk[s], BGk[s]], writes=[BpU[k]])
                    evac(Qv[:, cc * 2:cc * 2 + 2, :, k1], pu[:, 0:ncol].rearrange("p (r j) -> p r j", r=2),
                         reads=[BpU[k]], wacc=[BQst])
            sy.dma("sp", d["Qd"].rearrange("a p t -> p a t"), Qst[:], reads=[BQst], writes=[d["BQd"]], sem_of=BQst)
            sy.barrier()
        return False

    def phase_A1():
        rms_mode[0] = "lnexp"
        with ExitStack() as ph:
            P = ph.enter_context
            Wq = P(sbt("Wq", [128, 8, 512], BF16))
            Wk = P(sbt("Wk", [128, 8, 512], BF16))
            Wv = P(sbt("Wv", [128, 8, 512], BF16))
            BWq, BWk, BWv = sy.buf("Wq"), sy.buf("Wk"), sy.buf("Wv")
            load_w(Wk, w_in[:, 512:1024], BWk, 8)
            load_w(Wv, w_in[:, 1024:1536], BWv, 8)
            load_w(Wq, w_in[:, 0:512], BWq, 8)
            precast("g", w_in[:, 2048:4096], D, 2048)
            precast("na", w_na, 512, D)
            precast("fn", w_fn, 512, D)
            precast("out", w_out, D, D)
            precast("up", w_up, D, DFF)
            precast("dn", w_dn, DFF, D)
            NXS = 4
            xt = [P(sbt("a_xt%d" % i, [128, D], F32)) for i in range(NXS)]
            hb = [P(sbt("a_hb%d" % i, [128, D], BF16)) for i in range(NXS)]
            hT = [P(sbt("a_hT%d" % i, [128, 8, 512], BF16)) for i in range(2)]
            QT = [P(sbt("a_QT%d" % i, [128, 4, 512], BF16)) for i in range(2)]
            Kr = P(sbt("a_Kr", [128, 4, 16 * 128], BF16))
            Vr = P(sbt("a_Vr", [128, 16, NH, 65], BF16))
            KmT = P(sbt("a_KmT", [128, 4, 16], BF16))
            Vm = P(sbt("a_Vm", [16, NH, 65], BF16))
            EBg = P(sbt("a_EBg", [128, NH, 6, 128], BF16))
            EBs = [P(sbt("a_EBs%d" % i, [128, NH, 7, 128], BF16)) for i in range(2)]
            bst = [P(sbt("a_bst%d" % i, [128, 7 * 128], F32)) for i in range(2)]
            NE = 4
            SK = 3
            Et = [P(sbt("a_E%d" % i, [128, 7 * 128], BF16)) for i in range(NE)]
            Pt = [P(sbt("a_P%d" % i, [128, 7 * 128], BF16)) for i in range(NE)]
            osb = [P(sbt("a_o%d" % i, [128, 512], BF16)) for i in range(2)]
            rec = [P(sbt("a_rec%d" % i, [128, NH], F32)) for i in range(2)]
            oT = [P(sbt("a_oT%d" % i, [128, 4, 512], BF16)) for i in range(2)]
            Sps = P(pst("a_S", [128, 2048], F32))
            Opf = [P(pst("a_O%d" % i, [128, 512], F32)) for i in range(2)]
            Ops = [o[:, 0:260].rearrange("p (h e) -> p h e", e=65) for o in Opf]
            PXf = [P(pst("a_PX%d" % i, [128, 512], F32)) for i in range(1)]
            PXb = [P(pst("a_PT0", [128, 8, 128], BF16))]
            BPX = sy.bufs_n("aPX", 2)
            BPT = sy.bufs_n("aPT", 1)
            pxc = [0]
            Bxt = sy.bufs_n("axt", NXS)
            Bhb = sy.bufs_n("ahb", NXS)
            BhT, BQT, Bbst, Bosb, Brec, BoT, BS, BO, BEBs = (
                sy.bufs_n(n, 2) for n in ("ahT", "aQT", "abst", "aosb", "arec", "aoT", "aS", "aO", "aEBs"))
            BE = sy.bufs_n("aE", NE)
            BPt = sy.bufs_n("aP", NE)
            BK = sy.bufs_n("aK", 16)
            BV = sy.bufs_n("aV", 16)
            BKm, BVm, BEBg = (sy.buf(n) for n in ("aKm", "aVm", "aEBg"))
            BVones = sy.buf("aVones")
            sy.op("pool", lambda: G_.memset(Vr[:, :, :, 64:65], 1.0), writes=[BVones])
            sy.op("pool", lambda: G_.memset(Vm[:, :, 64:65], 1.0), writes=[BVones])
            for h in range(NH):
                s = h % 2
                sy.dma("sp", bst[s][:, 0:768], bgen[:, h, :, :].rearrange("p t q -> p (t q)"), writes=[Bbst[s]],
                       sem_of=Bbst[s])
                sy.op("act", lambda: A_.activation(out=EBg[:, h, :, :].rearrange("p t q -> p (t q)"),
                                                   in_=bst[s][:, 0:768], func=AF.Exp),
                      reads=[Bbst[s]], wacc=[BEBg])

            def xpose(h_sb, Bh, rows, dst_ap, Bdst, nchunk, wacc, eng=None):
                k = 0
                pT = PXb[k]

                def em():
                    i = None
                    for c in range(nchunk):
                        i = T_.transpose(out=pT[:, c, 0:rows], in_=h_sb[0:rows, c * 128:(c + 1) * 128],
                                         identity=ident[0:rows, 0:rows])
                    return i
                sy.op("pe", em, reads=[Bh, Bconst], writes=[BPT[k]])
                evac(dst_ap, pT[:, 0:nchunk, 0:rows], reads=[BPT[k]], wacc=[Bdst] if wacc else (),
                     writes=() if wacc else [Bdst], eng=eng)

            mmb = [(PXf[0], BPX[0]), (Opf[0], BO[0]), (Opf[1], BO[1])]

            def mm_group(nrow, ncol, lhs_fn, rhs_fn, reads):
                pm, Bpm_ = mmb[pxc[0] % 3]
                pxc[0] += 1

                def em():
                    i = None
                    for dc in range(8):
                        i = T_.matmul(pm[0:nrow, 0:ncol], lhsT=lhs_fn(dc), rhs=rhs_fn(dc), start=(dc == 0), stop=(dc == 7))
                    return i
                sy.op("pe", em, reads=reads, writes=[Bpm_])
                return pm, Bpm_

            sy.dma("sp", xt[0][0:16, :], meta[:, :], writes=[Bxt[0]], sem_of=Bxt[0])
            rmsnorm(xt[0][0:16, :], Bxt[0], 16, gbc_mix, hb[0][0:16, :], Bhb[0])
            xpose(hb[0], Bhb[0], 16, hT[0][:, :, 0:16], BhT[0], 8, False)
            for hp in range(4):
                pm, Bpm = mm_group(128, 16, lambda dc: Wk[:, dc, hp * 128:(hp + 1) * 128], lambda dc: hT[0][:, dc, 0:16],
                                   [BhT[0], BWk])
                evac(KmT[:, hp, :], pm[:, 0:16], reads=[Bpm], wacc=[BKm])
            pm, Bpm = mm_group(16, 512, lambda dc: hT[0][:, dc, 0:16], lambda dc: Wv[:, dc, :], [BhT[0], BWv])
            evac(Vm[0:16, :, 0:64], pm[0:16, :].rearrange("p (h d) -> p h d", d=64), reads=[Bpm, BVones], writes=[BVm])

            for g in (GP, GS):
                d = gd[g.name]
                xctr = [0]
                tslot = {}

                def ln_load(t):
                    s = xctr[0] % NXS
                    xctr[0] += 1
                    tslot[t] = s
                    sy.dma("sp", xt[s][:, :], d["xe"][128 * t:128 * t + 128, :], writes=[Bxt[s]], sem_of=Bxt[s])

                def ln_norm(t):
                    s = tslot[t]
                    rmsnorm(xt[s][:, :], Bxt[s], 128, gbc_mix, hb[s][:, :], Bhb[s])

                def ln(t):
                    ln_load(t)
                    ln_norm(t)

                def unit_tiles(u):
                    if u < 0:
                        return [0, 1]
                    if u < g.NB:
                        return [4 * u + 2 + i for i in range(4)]
                    if u == g.NB:
                        return [g.NP + 2, g.NP + 3]
                    return []

                def qkv_mm(u):
                    ts = unit_tiles(u)
                    if not ts:
                        return
                    with_q = 0 <= u < g.NB
                    hs = u % 2
                    ncols = 128 * len(ts)
                    for i, t in enumerate(ts):
                        s = tslot[t]
                        xpose(hb[s], Bhb[s], 128, hT[hs][:, :, i * 128:(i + 1) * 128], BhT[hs], 8, i > 0)
                    for hp in range(4):
                        pm, Bpm = mm_group(128, ncols, lambda dc: Wk[:, dc, hp * 128:(hp + 1) * 128],
                                           lambda dc: hT[hs][:, dc, 0:ncols], [BhT[hs], BWk])
                        e_ = "act" if hp % 2 == 0 else "dve"
                        for i, t in enumerate(ts):
                            sl = t % 16
                            evac(Kr[:, hp, sl * 128:(sl + 1) * 128], pm[:, i * 128:(i + 1) * 128], reads=[Bpm],
                                 wacc=[BK[sl]] if hp > 0 else (), writes=[BK[sl]] if hp == 0 else (), eng=e_)
                    if with_q:
                        for hp in range(4):
                            pm, Bpm = mm_group(128, ncols, lambda dc: Wq[:, dc, hp * 128:(hp + 1) * 128],
                                               lambda dc: hT[hs][:, dc, 0:ncols], [BhT[hs], BWq])
                            evac(QT[hs][:, hp, 0:ncols], pm[:, 0:ncols], reads=[Bpm],
                                 wacc=[BQT[hs]] if hp > 0 else (), writes=[BQT[hs]] if hp == 0 else ())
                    for i, t in enumerate(ts):
                        sl = t % 16
                        pm, Bpm = mm_group(128, 512, lambda dc: hT[hs][:, dc, i * 128:(i + 1) * 128],
                                           lambda dc: Wv[:, dc, :], [BhT[hs], BWv])
                        evac(Vr[:, sl, :, 0:64], pm[:, :].rearrange("p (h d) -> p h d", d=64), reads=[Bpm, BVones],
                             writes=[BV[sl]])
                    if with_q:
                        sy.dma("sp", d["hTd"][:, :, u * 512:(u + 1) * 512].rearrange("a p t -> p a t"), hT[hs][:],
                               reads=[BhT[hs]], wacc=[d["BhTd"]], sem_of=BhT[hs])

                specials = {p: (idx, t0, nt) for idx, (p, t0, nt) in enumerate(_special_pairs(g))}

                sp_q = []
                sp_staged = []
                sp_ctr = [0]

                def sp_enqueue(B):
                    nsp = 0
                    for p in range(4 * B, 4 * B + 4):
                        if p in specials:
                            for h in range(NH):
                                sp_q.append((p, nsp, h))
                            nsp += 1

                def sp_step(n=1):
                    for _ in range(n):
                        if sp_q:
                            p, slot, h = sp_q.pop(0)
                            idx, t0, nt = specials[p]
                            s_ = sp_ctr[0] % 2
                            sp_ctr[0] += 1
                            w_ = (nt + 1) * 128
                            sy.dma("sp", bst[s_][:, 0:w_], d["bs"][idx, :, h, 0:nt + 1, :].rearrange("p t q -> p (t q)"),
                                   writes=[Bbst[s_]], sem_of=Bbst[s_])
                            sp_staged.append((p, slot, h, s_))
                        if len(sp_staged) > 1 or (sp_staged and not sp_q):
                            p, slot, h, s_ = sp_staged.pop(0)
                            idx, t0, nt = specials[p]
                            w_ = (nt + 1) * 128
                            sy.op("act", lambda: A_.activation(out=EBs[slot][:, h, 0:nt + 1, :].rearrange("p t q -> p (t q)"),
                                                               in_=bst[s_][:, 0:w_], func=AF.Exp),
                                  reads=[Bbst[s_]], wacc=[BEBs[slot]] if h > 0 else (), writes=[BEBs[slot]] if h == 0 else ())

                def sp_flush():
                    while sp_q or sp_staged:
                        sp_step(1)

                def att_block(B, tiles_next):
                    pairs = list(range(4 * B, 4 * B + 4))
                    qs = B % 2
                    pinfo = {}
                    nsp = 0
                    for p in pairs:
                        if p in specials:
                            idx, t0, nt = specials[p]
                            pinfo[p] = (t0, nt, EBs[nsp], BEBs[nsp])
                            nsp += 1
                        else:
                            pinfo[p] = (p, 5, EBg, BEBg)
                    if B + 1 == g.NB - 1:
                        sp_enqueue(B + 1)
                    items = [(p, h) for p in pairs for h in range(NH)]
                    pend_T = []

                    def emit_S(k):
                        p, h = items[k]
                        t0, nt, EB, BEB = pinfo[p]
                        tiles = list(range(t0, t0 + nt))
                        qc = slice((p % 4) * 128, (p % 4 + 1) * 128)
                        s = k % 2
                        s3 = k % NE
                        se = k % NE
                        hp = h // 2
                        pr = (h % 2) * 64
                        S = Sps[:, s * 1024:s * 1024 + 896]
                        BSs = [BS[s]]
                        me = "pool" if k % 2 == 0 else "dve"
                        ME = G_ if k % 2 == 0 else V_

                        def em():
                            for i, t in enumerate(tiles):
                                sl = t % 16
                                T_.matmul(S[:, i * 128:(i + 1) * 128], lhsT=Kr[pr:pr + 64, hp, sl * 128:(sl + 1) * 128],
                                          rhs=QT[qs][pr:pr + 64, hp, qc], start=True, stop=True)
                            return T_.matmul(S[0:16, nt * 128:(nt + 1) * 128], lhsT=KmT[pr:pr + 64, hp, :],
                                             rhs=QT[qs][pr:pr + 64, hp, qc], start=True, stop=True)
                        sy.op("pe", em, reads=[BQT[qs], BKm] + [BK[t % 16] for t in tiles], writes=BSs)
                        sy.op("act", lambda: A_.activation(out=Et[se][:, 0:nt * 128], in_=S[:, 0:nt * 128], func=AF.Exp,
                                                           scale=DH ** -0.5), reads=BSs, writes=[BE[se]])
                        sy.op("act", lambda: A_.activation(out=Et[se][0:16, nt * 128:(nt + 1) * 128],
                                                           in_=S[0:16, nt * 128:(nt + 1) * 128], func=AF.Exp,
                                                           scale=DH ** -0.5), reads=BSs, wacc=[BE[se]])
                        sy.op(me, lambda: ME.tensor_tensor(out=Pt[s3][:, 0:nt * 128], in0=Et[se][:, 0:nt * 128],
                                                           in1=EB[:, h, 0:nt, :].rearrange("p t q -> p (t q)"),
                                                           op=ALU.mult), reads=[BE[se], BEB], writes=[BPt[s3]])
                        sy.op(me, lambda: ME.tensor_tensor(out=Pt[s3][0:16, nt * 128:(nt + 1) * 128],
                                                           in0=Et[se][0:16, nt * 128:(nt + 1) * 128],
                                                           in1=EB[0:16, h, nt, :], op=ALU.mult),
                              reads=[BE[se], BEB], wacc=[BPt[s3]])

                    def emit_PV(k):
                        p, h = items[k]
                        t0, nt, EB, BEB = pinfo[p]
                        tiles = list(range(t0, t0 + nt))
                        s3 = k % NE
                        O = Ops[h // 4]
                        hh = h % 4

                        def em():
                            for i, t in enumerate(tiles):
                                T_.matmul(O[:, hh, :], lhsT=Pt[s3][:, i * 128:(i + 1) * 128], rhs=Vr[:, t % 16, h, :],
                                          start=(i == 0), stop=False)
                            return T_.matmul(O[:, hh, :], lhsT=Pt[s3][0:16, nt * 128:(nt + 1) * 128], rhs=Vm[0:16, h, :],
                                             start=False, stop=True)
                        sy.op("pe", em, reads=[BPt[s3], BVm] + [BV[t % 16] for t in tiles],
                              wacc=[BO[h // 4]] if hh > 0 else (), writes=[BO[h // 4]] if hh == 0 else ())
                        if hh == 3:
                            a = h // 4
                            osl = p % 2
                            sy.op("dve", lambda: V_.reciprocal(out=rec[osl][:, a * 4:(a + 1) * 4].unsqueeze(2),
                                                               in_=Ops[a][:, :, 64:65]),
                                  reads=[BO[a]], wacc=[Brec[osl]] if a else (), writes=() if a else [Brec[osl]])
                            sy.op("dve", lambda: V_.tensor_tensor(
                                out=osb[osl][:, a * 256:(a + 1) * 256].rearrange("p (h d) -> p h d", d=64),
                                in0=Ops[a][:, :, 0:64],
                                in1=rec[osl][:, a * 4:(a + 1) * 4].unsqueeze(2).to_broadcast([128, 4, 64]),
                                op=ALU.mult), reads=[BO[a], Brec[osl]], wacc=[Bosb[osl]] if a else (),
                                writes=() if a else [Bosb[osl]])
                            if a == 1:
                                pend_T.append([p, 2])

                    def emit_T(p):
                        osl = p % 2
                        i4 = p % 4
                        qc = slice(i4 * 128, (i4 + 1) * 128)
                        xpose(osb[osl], Bosb[osl], 128, oT[qs][:, :, qc], BoT[qs], 4, i4 > 0)
                        if i4 == 3:
                            sy.dma("sp", d["oTd"][:, :, B * 512:(B + 1) * 512].rearrange("a p t -> p a t"), oT[qs][:],
                                   reads=[BoT[qs]], wacc=[d["BoTd"]], sem_of=BoT[qs])

                    n = len(items)
                    if tiles_next:
                        ln_load(tiles_next[0])
                    for k in range(n + SK):
                        if k % 2 == 0:
                            sp_step(1)
                        if k < n:
                            emit_S(k)
                            if items[k][1] == NH - 1:
                                i4 = items[k][0] % 4
                                if i4 < len(tiles_next):
                                    ln_norm(tiles_next[i4])
                                if i4 + 1 < len(tiles_next):
                                    ln_load(tiles_next[i4 + 1])
                                precast_some(2)
                        if k >= SK:
                            emit_PV(k - SK)
                        for e in list(pend_T):
                            if e[1] == 0:
                                emit_T(e[0])
                                pend_T.remove(e)
                            else:
                                e[1] -= 1
                    for e in pend_T:
                        emit_T(e[0])
                    sp_flush()

                sp_enqueue(0)
                for t in unit_tiles(-1):
                    sp_step(2)
                    ln(t)
                qkv_mm(-1)
                for t in unit_tiles(0):
                    sp_step(2)
                    ln(t)
                qkv_mm(0)
                for t in unit_tiles(1):
                    sp_step(2)
                    ln(t)
                qkv_mm(1)
                sp_flush()
                for B in range(g.NB):
                    att_block(B, unit_tiles(B + 2))
                    qkv_mm(B + 2)
            precast_some(len(precast_q))
            sy.barrier()

    def phase_A2():
        rms_mode[0] = "sqrt"
        with ExitStack() as ph:
            P = ph.enter_context
            Wg = P(sbt("Wg", [128, 8, 2048], BF16))
            Wna = P(sbt("Wna", [128, 4, D], BF16))
            Wfn = P(sbt("Wfn", [128, 4, D], BF16))
            Wf2 = P(sbt("Wf2", [128, 8, D], BF16))
            Wo = P(sbt("Wo", [128, 8, D], BF16))
            cs = P(sbt("cs_sb", [128, 2, 128], BF16))
            BWg, BWna, BWfn, BWf2, BWo, Bcs = (sy.buf(n) for n in ("Wg", "Wna", "Wfn", "Wf2", "Wo", "cs"))
            sy.dma("sp", cs[:], csd[:, :, :], writes=[Bcs], sem_of=Bcs)
            load_w(Wfn, wb["fn"][0], BWfn, 4, wb["fn"][1])
            load_w(Wg, wb["g"][0], BWg, 8, wb["g"][1])
            load_w(Wna, wb["na"][0], BWna, 4, wb["na"][1])
            load_w(Wo, wb["out"][0], BWo, 8, wb["out"][1])
            ps = [P(pst("b_ps%d" % i, [128, 512], F32)) for i in range(8)]
            Bps = sy.bufs_n("bps", 8)
            pctr = [0]

            def nextps():
                k = pctr[0] % 8
                pctr[0] += 1
                return ps[k], Bps[k]
            for cc in range(4):
                for ri in range(2):
                    for half in range(2):
                        p_, Bp_ = nextps()
                        sy.op("pe", lambda: T_.matmul(p_[:, :], lhsT=cs[:, ri, :], rhs=Wfn[:, cc, half * 512:(half + 1) * 512],
                                                      start=True, stop=True), reads=[Bcs, BWfn], writes=[Bp_])
                        evac(Wf2[:, cc * 2 + ri, half * 512:(half + 1) * 512], p_[:, :], reads=[Bp_], wacc=[BWf2])
            hT = [P(sbt("b_hT%d" % i, [128, 8, 512], BF16)) for i in range(2)]
            oT = [P(sbt("b_oT%d" % i, [128, 4, 512], BF16)) for i in range(2)]
            QT = [P(sbt("b_QT%d" % i, [128, 8, 512], BF16)) for i in range(2)]
            xr = [P(sbt("b_xr%d" % i, [128, D], F32)) for i in range(2)]
            sg = [P(sbt("b_sg%d" % i, [128, 512], F32)) for i in range(4)]
            tm = [P(sbt("b_tm%d" % i, [128, 512], F32)) for i in range(4)]
            mx = [P(sbt("b_mx%d" % i, [128, 8, 512], BF16)) for i in range(2)]
            x1 = [P(sbt("b_x1%d" % i, [128, D], F32)) for i in range(2)]
            BhT, BoT, BQT, Bxr, Bmx, Bx1 = (sy.bufs_n(n, 2) for n in ("bhT", "boT", "bQT", "bxr", "bmx", "bx1"))
            Bsg = sy.bufs_n("bsg", 4)
            Btm = sy.bufs_n("btm", 4)
            x1c = [0]
            for g in (GP, GS):
                d = gd[g.name]

                def loads(B):
                    s = B % 2
                    c = slice(B * 512, (B + 1) * 512)
                    sy.dma("sp", hT[s][:], d["hTd"][:, :, c].rearrange("a p t -> p a t"), reads=[d["BhTd"]],
                           writes=[BhT[s]], sem_of=BhT[s])
                    sy.dma("sp", oT[s][:], d["oTd"][:, :, c].rearrange("a p t -> p a t"), reads=[d["BoTd"]],
                           writes=[BoT[s]], sem_of=BoT[s])
                    sy.dma("sp", QT[s][:], d["Qd"][:, :, c].rearrange("a p t -> p a t"), reads=[d["BQd"]],
                           writes=[BQT[s]], sem_of=BQT[s])
                loads(0)
                for B in range(g.NB):
                    s = B % 2
                    if B + 1 < g.NB:
                        loads(B + 1)
                    for dm in range(8):
                        dsl = slice(dm * 128, (dm + 1) * 128)
                        k = dm % 2
                        for gi in range(2):
                            p_, Bp_ = nextps()

                            def em():
                                i = None
                                for dc in range(8):
                                    i = T_.matmul(p_[:, :], lhsT=Wg[:, dc, gi * 1024 + dm * 128:gi * 1024 + (dm + 1) * 128],
                                                  rhs=hT[s][:, dc, :], start=(dc == 0), stop=(dc == 7))
                                return i
                            sy.op("pe", em, reads=[BhT[s], BWg], writes=[Bp_])
                            sy.op("act", lambda: A_.activation(out=sg[k * 2 + gi][:, :], in_=p_[:, :], func=AF.Sigmoid),
                                  reads=[Bp_], writes=[Bsg[k * 2 + gi]])
                        p1, Bp1 = nextps()

                        def em():
                            i = None
                            for c4 in range(4):
                                i = T_.matmul(p1[:, :], lhsT=Wna[:, c4, dsl], rhs=oT[s][:, c4, :], start=(c4 == 0), stop=(c4 == 3))
                            return i
                        sy.op("pe", em, reads=[BoT[s], BWna], writes=[Bp1])
                        p2, Bp2 = nextps()

                        def em():
                            i = None
                            for c8 in range(8):
                                i = T_.matmul(p2[:, :], lhsT=Wf2[:, c8, dsl], rhs=QT[s][:, c8, :], start=(c8 == 0), stop=(c8 == 7))
                            return i
                        sy.op("pe", em, reads=[BQT[s], BWf2], writes=[Bp2])
                        sy.op("dve", lambda: V_.tensor_tensor(out=tm[k * 2][:, :], in0=p1[:, :], in1=sg[k * 2][:, :], op=ALU.mult),
                              reads=[Bp1, Bsg[k * 2]], writes=[Btm[k * 2]])
                        sy.op("dve", lambda: V_.tensor_tensor(out=tm[k * 2 + 1][:, :], in0=p2[:, :], in1=sg[k * 2 + 1][:, :], op=ALU.mult),
                              reads=[Bp2, Bsg[k * 2 + 1]], writes=[Btm[k * 2 + 1]])
                        sy.op("pool", lambda: G_.tensor_tensor(out=mx[s][:, dm, :], in0=tm[k * 2][:, :], in1=tm[k * 2 + 1][:, :], op=ALU.add),
                              reads=[Btm[k * 2], Btm[k * 2 + 1]], wacc=[Bmx[s]] if dm > 0 else (), writes=[Bmx[s]] if dm == 0 else ())
                    for tt in range(4):
                        xs_ = x1c[0] % 2
                        x1c[0] += 1
                        r0 = B * 512 + tt * 128
                        if B == 0 and tt == 0:
                            sy.dma("sp", xr[xs_][:, :], d["xe"][256 + r0:256 + r0 + 128, :], writes=[Bxr[xs_]], sem_of=Bxr[xs_])
                        r1 = r0 + 128
                        if r1 < g.Th:
                            xn_ = x1c[0] % 2
                            sy.dma("sp", xr[xn_][:, :], d["xe"][256 + r1:256 + r1 + 128, :], writes=[Bxr[xn_]], sem_of=Bxr[xn_])
                        for half in range(2):
                            p_, Bp_ = nextps()

                            def em():
                                i = None
                                for dm in range(8):
                                    i = T_.matmul(p_[:, :], lhsT=mx[s][:, dm, tt * 128:(tt + 1) * 128],
                                                  rhs=Wo[:, dm, half * 512:(half + 1) * 512], start=(dm == 0), stop=(dm == 7))
                                return i
                            sy.op("pe", em, reads=[Bmx[s], BWo], writes=[Bp_])
                            sy.op("dve", lambda: V_.tensor_tensor(out=x1[xs_][:, half * 512:(half + 1) * 512], in0=p_[:, :],
                                                                  in1=xr[xs_][:, half * 512:(half + 1) * 512], op=ALU.add),
                                  reads=[Bp_, Bxr[xs_]], wacc=[Bx1[xs_]] if half else (), writes=() if half else [Bx1[xs_]])
                        r0 = B * 512 + tt * 128
                        sy.dma("sp", d["x1d"][r0:r0 + 128, :], x1[xs_][:, :], reads=[Bx1[xs_]], wacc=[d["Bx1d"]], sem_of=Bx1[xs_])
            sy.barrier()

    def phase_B():
        TB = 256
        NT = TB // 128
        with ExitStack() as ph:
            P = ph.enter_context
            Wup = P(sbt("Wup", [128, 8, DFF], BF16))
            Wdn = P(sbt("Wdn", [128, 32, D], BF16))
            BWup, BWdn = sy.buf("Wup"), sy.buf("Wdn")
            load_w(Wup, wb["up"][0], BWup, 8, wb["up"][1])
            load_w(Wdn, wb["dn"][0], BWdn, 32, wb["dn"][1])
            ps = [P(pst("c_ps%d" % i, [128, 512], F32)) for i in range(6)]
            Bps = sy.bufs_n("cps", 6)
            pT = [P(pst("c_pT%d" % i, [128, 8, 128], BF16)) for i in range(2)]
            BpT = sy.bufs_n("cpT", 2)
            pctr = [0]

            def nextps():
                k = pctr[0] % 6
                pctr[0] += 1
                return ps[k], Bps[k]
            x1 = [P(sbt("c_x1%d" % i, [128, NT, D], F32)) for i in range(3)]
            hb = [P(sbt("c_hb%d" % i, [128, D], BF16)) for i in range(2)]
            hT = [P(sbt("c_hT%d" % i, [128, 8, TB], BF16)) for i in range(2)]
            rl = [P(sbt("c_rl%d" % i, [128, TB], BF16)) for i in range(2)]
            aT = P(sbt("c_aT", [128, 32, TB], BF16))
            x2 = [P(sbt("c_x2%d" % i, [128, D], F32)) for i in range(2)]
            Bhb, BhT, Brl, Bx2 = (sy.bufs_n(n, 2) for n in ("chb", "chT", "crl", "cx2"))
            Bx1 = sy.bufs_n("cx1", 3)
            BaT = sy.bufs_n("caT", 32)
            c2 = [0]
            for g in (GP, GS):
                d = gd[g.name]
                nblk = g.Th // TB

                def loads(B):
                    if B >= nblk:
                        return
                    s = B % 3
                    sy.dma("sp", x1[s][:], d["x1d"][B * TB:(B + 1) * TB, :].rearrange("(t p) f -> p t f", p=128),
                           reads=[d["Bx1d"]], writes=[Bx1[s]], sem_of=Bx1[s])

                def stage_N(B):
                    if B >= nblk:
                        return
                    for tt in range(NT):
                        rmsnorm(x1[B % 3][:, tt, :], Bx1[B % 3], 128, gbc_mlp, hb[tt][:, :], Bhb[tt])

                def stage_X(B):
                    if B >= nblk:
                        return
                    for tt in range(NT):
                        transpose_to(pT[tt], BpT[tt], hb[tt], Bhb[tt], 128, lambda: hT[B % 2][:, :, tt * 128:(tt + 1) * 128],
                                     BhT[B % 2], wacc=(tt > 0))

                def stage_up(B):
                    s = B % 2
                    for fc in range(32):
                        p_, Bp_ = nextps()

                        def em():
                            i = None
                            for dc in range(8):
                                i = T_.matmul(p_[:, 0:TB], lhsT=Wup[:, dc, fc * 128:(fc + 1) * 128], rhs=hT[s][:, dc, :],
                                              start=(dc == 0), stop=(dc == 7))
                            return i
                        sy.op("pe", em, reads=[BhT[s], BWup], writes=[Bp_])
                        k = fc % 2
                        sy.op("act", lambda: A_.activation(out=rl[k][:, :], in_=p_[:, 0:TB], func=AF.Relu),
                              reads=[Bp_], writes=[Brl[k]])
                        sy.op("pool", lambda: G_.tensor_tensor(out=aT[:, fc, :], in0=rl[k][:, :], in1=rl[k][:, :], op=ALU.mult),
                              reads=[Brl[k]], writes=[BaT[fc]])
                        if fc == 7:
                            stage_final(B - 1)
                            stage_N(B + 1)

                pend_final = {}

                def stage_final(B):
                    for (xs_, r0) in pend_final.pop(B, []):
                        rmsnorm(x2[xs_][:, :], Bx2[xs_], 128, gbc_fin, x2[xs_][:, :], Bx2[xs_])
                        sy.dma("sp", d["y"][r0:r0 + 128, :], x2[xs_][:, :], reads=[Bx2[xs_]], wacc=[d["By"]], sem_of=Bx2[xs_])

                def stage_down(B):
                    s = B % 3
                    for tt in range(NT):
                        xs_ = c2[0] % 2
                        c2[0] += 1
                        for half in range(2):
                            p_, Bp_ = nextps()

                            def em():
                                i = None
                                for fc in range(32):
                                    i = T_.matmul(p_[:, :], lhsT=aT[:, fc, tt * 128:(tt + 1) * 128],
                                                  rhs=Wdn[:, fc, half * 512:(half + 1) * 512], start=(fc == 0), stop=(fc == 31))
                                return i
                            sy.op("pe", em, reads=BaT + [BWdn], writes=[Bp_])
                            sy.op("dve", lambda: V_.tensor_tensor(out=x2[xs_][:, half * 512:(half + 1) * 512], in0=p_[:, :],
                                                                  in1=x1[s][:, tt, half * 512:(half + 1) * 512], op=ALU.add),
                                  reads=[Bp_, Bx1[s]], wacc=[Bx2[xs_]] if half else (), writes=() if half else [Bx2[xs_]])
                        pend_final.setdefault(B, []).append((xs_, B * TB + tt * 128))

                loads(0)
                loads(1)
                stage_N(0)
                stage_X(0)
                for B in range(nblk):
                    stage_up(B)
                    stage_X(B + 1)
                    stage_down(B)
                    loads(B + 2)
                stage_final(nblk - 1)
            sy.barrier()

    done = False
    for g in (GP, GS):
        if phase_F(g):
            done = True
            break
    if not done and stop_after != "F":
        phase_A1()
        if stop_after != "A1":
            phase_A2()
            if stop_after != "A2":
                phase_B()
    sy.barrier(engines=["sp"])
    top.close()
    return nc


_NC_CACHE = {}


def _prep_inputs(x_prompt, x_sample, meta_tokens, w_in, rel_bias, meta_bias, w_branch_na, w_branch_fn,
                 w_out, g_mix, g_mlp, w_up, w_down, g_final):
    f = lambda a: np.ascontiguousarray(np.asarray(a, dtype=np.float32))
    ident, w16, cs = _host_consts()
    rb = f(rel_bias)[0]
    mb = f(meta_bias)[0]
    meta = f(meta_tokens)
    common = dict(
        w_in=f(w_in)[0], w_na=f(w_branch_na)[0], w_fn=f(w_branch_fn)[0], w_out=f(w_out)[0], w_up=f(w_up)[0],
        w_dn=f(w_down)[0], g_mix=f(g_mix).reshape(1, D), g_mlp=f(g_mlp).reshape(1, D), g_fin=f(g_final).reshape(1, D),
        meta=meta, ident=ident, w16=w16, cs=cs,
        bgen=_bias_tiles(rb, mb, GP, 0, 16, 12, 5),
    )
    xs = dict(p=f(x_prompt), s=f(x_sample))
    Gc = {}
    in_maps = []
    for c in range(8):
        b, hf = c // 2, c % 2
        m = dict(common)
        for g in (GP, GS):
            n = g.name
            x = xs[n][b]
            m["xs_" + n] = np.concatenate([meta, x], 0)
            xe = np.zeros((g.next, D), np.float32)
            lo = (g.R * hf - 4) * GW
            hi = lo + g.next
            a, bnd = max(lo, 0), min(hi, g.T)
            xe[a - lo:bnd - lo] = x[a:bnd]
            m["xe_" + n] = xe
            if (n, hf) not in Gc:
                Gc[(n, hf)] = _host_G(g, hf) + (np.stack([
                    np.pad(_bias_tiles(rb, mb, g, hf, 2 * p, 2 * t0 - 4, nt), ((0, 0), (0, 0), (0, 6 - nt), (0, 0)),
                           constant_values=NEG) for (p, t0, nt) in _special_pairs(g)]),)
            m["gm_" + n], m["gl_" + n], m["bs_" + n] = Gc[(n, hf)]
        in_maps.append(m)
    return in_maps


def kernel(x_prompt, x_sample, meta_tokens, w_in, rel_bias, meta_bias, w_branch_na, w_branch_fn,
           w_out, g_mix, g_mlp, w_up, w_down, g_final):
    in_maps = _prep_inputs(x_prompt, x_sample, meta_tokens, w_in, rel_bias, meta_bias, w_branch_na, w_branch_fn,
                           w_out, g_mix, g_mlp, w_up, w_down, g_final)
    if "nc" not in _NC_CACHE:
        _NC_CACHE["nc"] = build_nc()
    res = run_bass_kernel_spmd(_NC_CACHE["nc"], in_maps, core_ids=list(range(8)))
    yp = np.zeros((4, GP.T, D), np.float32)
    ys = np.zeros((4, GS.T, D), np.float32)
    for c in range(8):
        b, hf = c // 2, c % 2
        r = res.results[c]
        yp[b, hf * GP.Th:(hf + 1) * GP.Th] = r["y_p"]
        ys[b, hf * GS.Th:(hf + 1) * GS.Th] = r["y_s"]
    return (yp, ys)
```
